# Optimizing a Trainium2 kernel written in Bass

```python
import jax, jax.numpy as jnp
from jax import lax
import numpy as np


D_MODEL = 1024
BATCH = 8
SEQ = 2048
DEPTH = 1

CHUNK = 64
Q_BLOCK = 128
D_A = D_MODEL // 2
HEAD_A = 64
H_A = D_A // HEAD_A
DECAY_RANK = 64
AAA_RANK = 64
D_B = D_MODEL // 2
HEAD_B = 64
H_B = D_B // HEAD_B
RWKV_COLS = 4 * D_A + DECAY_RANK + AAA_RANK
FOX_COLS = 4 * D_B + H_B
GATE_COLS = 2 * D_MODEL
IN_COLS = RWKV_COLS + FOX_COLS + GATE_COLS
RMS_EPS = 1e-6
LNX_EPS = 64e-5

kernel_name = 'hybrid_rwkv7_fox_gated_block'


def _rmsnorm(x, g):
    xf = x.astype(jnp.float32)
    y = xf * lax.rsqrt(jnp.mean(xf * xf, axis=-1, keepdims=True) + RMS_EPS)
    return (y * g.astype(jnp.float32)).astype(x.dtype)


def _token_shift(u):
    return jnp.pad(u, ((0, 0), (1, 0), (0, 0)))[:, :-1]


def _wkv7_scan(r, decay, k, v, a_vec, b_vec):
    B, S, H, N = r.shape

    def step(state, inp):
        r_t, w_t, k_t, v_t, a_t, b_t = inp
        sa = jnp.einsum('bhij,bhj->bhi', state, a_t)
        state = (state * w_t[:, :, None, :]
                 + sa[..., None] * b_t[:, :, None, :]
                 + v_t[..., None] * k_t[:, :, None, :])
        return state, jnp.einsum('bhij,bhj->bhi', state, r_t)

    xs = (jnp.moveaxis(r, 1, 0), jnp.moveaxis(decay, 1, 0), jnp.moveaxis(k, 1, 0),
          jnp.moveaxis(v, 1, 0), jnp.moveaxis(a_vec, 1, 0), jnp.moveaxis(b_vec, 1, 0))
    state0 = jnp.zeros((B, H, N, N), jnp.float32)
    _, y = lax.scan(step, state0, xs)
    return jnp.moveaxis(y, 0, 1)


def _rwkv7_mixer(u, mu, w_up, w0, a_up, a0, k_k, k_a, r_k, lnx_w, lnx_b):
    B, S, _ = u.shape
    u = u + (_token_shift(u) - u) * mu
    r, k, v, wd, ad, gate = jnp.split(
        u, [D_A, 2 * D_A, 3 * D_A, 3 * D_A + DECAY_RANK, 3 * D_A + DECAY_RANK + AAA_RANK], axis=-1)
    w = -jax.nn.softplus(-(w0 + jnp.tanh(wd) @ w_up)) - 0.5
    decay = jnp.exp(-jnp.exp(w.astype(jnp.float32)))
    a = jax.nn.sigmoid(a0 + ad @ a_up)
    heads = lambda t: t.reshape(B, S, H_A, HEAD_A).astype(jnp.float32)
    kk = heads(k * k_k)
    kk = kk / jnp.maximum(jnp.linalg.norm(kk, axis=-1, keepdims=True), 1e-12)
    k = k * (1.0 + (a - 1.0) * k_a)
    r_h, k_h, v_h, a_h = heads(r), heads(k), heads(v), heads(a)
    y = _wkv7_scan(r_h, heads(decay), k_h, v_h, -kk, kk * a_h)
    mean = jnp.mean(y, axis=-1, keepdims=True)
    var = jnp.mean(jnp.square(y - mean), axis=-1, keepdims=True)
    y = (y - mean) * lax.rsqrt(var + LNX_EPS)
    y = y * lnx_w.reshape(H_A, HEAD_A).astype(jnp.float32) + lnx_b.reshape(H_A, HEAD_A).astype(jnp.float32)
    bonus = jnp.sum(r_h * k_h * r_k.astype(jnp.float32), axis=-1, keepdims=True) * v_h
    y = (y + bonus).reshape(B, S, D_A).astype(u.dtype)
    return y * jax.nn.silu(gate)


def _fox_mixer(u, f_bias, q_norm_g, k_norm_g):
    B, S, _ = u.shape
    q, k, v, gate, f_logit = jnp.split(u, [D_B, 2 * D_B, 3 * D_B, 4 * D_B], axis=-1)
    to_bhsd = lambda t: jnp.transpose(t, (0, 2, 1, 3))
    q = to_bhsd(_rmsnorm(q.reshape(B, S, H_B, HEAD_B), q_norm_g))
    k = to_bhsd(_rmsnorm(k.reshape(B, S, H_B, HEAD_B), k_norm_g))
    v = to_bhsd(v.reshape(B, S, H_B, HEAD_B))
    log_f = jax.nn.log_sigmoid((f_logit + f_bias).astype(jnp.float32))
    cum = jnp.transpose(jnp.cumsum(log_f, axis=1), (0, 2, 1))
    scale = HEAD_B ** -0.5
    outs = []
    for i in range(S // Q_BLOCK):
        lo, hi = i * Q_BLOCK, (i + 1) * Q_BLOCK
        qb, kp, vp = q[:, :, lo:hi], k[:, :, :hi], v[:, :, :hi]
        logits = (jnp.einsum('bhqd,bhkd->bhqk', qb, kp).astype(jnp.float32) * scale
                  + cum[:, :, lo:hi, None] - cum[:, :, None, :hi])
        causal = (lo + jnp.arange(Q_BLOCK))[:, None] >= jnp.arange(hi)[None, :]
        logits = jnp.where(causal, logits, -jnp.inf)
        p = jax.nn.softmax(logits, axis=-1)
        outs.append(jnp.einsum('bhqk,bhkd->bhqd', p.astype(vp.dtype), vp))
    o = jnp.concatenate(outs, axis=2)
    o = jnp.transpose(o, (0, 2, 1, 3)).reshape(B, S, D_B)
    return o * jax.nn.silu(gate)


def setup_inputs(seed: int = 0) -> dict:
    key = jax.random.key(seed)
    ks = jax.random.split(key, 20)
    nrm = lambda k, shape, s: jax.random.normal(k, shape, jnp.float32) * s
    unif = lambda k, shape, lo, hi: jax.random.uniform(k, shape, jnp.float32, lo, hi)
    return {
        'x': jax.random.normal(ks[0], (BATCH, SEQ, D_MODEL), jnp.float32),
        'norm_g': 1.0 + nrm(ks[1], (DEPTH, D_MODEL), 0.02),
        'w_in': nrm(ks[2], (DEPTH, D_MODEL, IN_COLS), D_MODEL ** -0.5),
        'shift_mu': unif(ks[3], (DEPTH, RWKV_COLS), 0.0, 1.0),
        'w_lora_up': nrm(ks[4], (DEPTH, DECAY_RANK, D_A), 0.1 * DECAY_RANK ** -0.5),
        'w0': unif(ks[5], (DEPTH, D_A), -6.0, -1.0),
        'a_lora_up': nrm(ks[6], (DEPTH, AAA_RANK, D_A), 0.1 * AAA_RANK ** -0.5),
        'a0': nrm(ks[7], (DEPTH, D_A), 0.5),
        'k_k': 0.85 + nrm(ks[8], (DEPTH, D_A), 0.02),
        'k_a': 1.0 + nrm(ks[9], (DEPTH, D_A), 0.02),
        'r_k': nrm(ks[10], (DEPTH, H_A, HEAD_A), 0.1),
        'lnx_w': 1.0 + nrm(ks[11], (DEPTH, D_A), 0.02),
        'lnx_b': nrm(ks[12], (DEPTH, D_A), 0.02),
        'f_bias': unif(ks[13], (DEPTH, H_B), 1.0, 5.0),
        'q_norm_g': 1.0 + nrm(ks[14], (DEPTH, HEAD_B), 0.02),
        'k_norm_g': 1.0 + nrm(ks[15], (DEPTH, HEAD_B), 0.02),
        'w_out_a': nrm(ks[16], (DEPTH, D_A, D_MODEL), D_A ** -0.5),
        'w_out_b': nrm(ks[17], (DEPTH, D_B, D_MODEL), D_B ** -0.5),
        'w_out': nrm(ks[18], (DEPTH, D_MODEL, D_MODEL), D_MODEL ** -0.5),
        'final_norm_g': 1.0 + nrm(ks[19], (D_MODEL,), 0.02),
    }


def reference(x, norm_g, w_in, shift_mu, w_lora_up, w0, a_lora_up, a0, k_k, k_a, r_k,
              lnx_w, lnx_b, f_bias, q_norm_g, k_norm_g, w_out_a, w_out_b, w_out, final_norm_g):
    for l in range(DEPTH):
        h = _rmsnorm(x, norm_g[l])
        u = h @ w_in[l]
        u_a, u_b, u_g = jnp.split(u, [RWKV_COLS, RWKV_COLS + FOX_COLS], axis=-1)
        y_a = _rwkv7_mixer(u_a, shift_mu[l], w_lora_up[l], w0[l], a_lora_up[l], a0[l],
                           k_k[l], k_a[l], r_k[l], lnx_w[l], lnx_b[l]) @ w_out_a[l]
        y_b = _fox_mixer(u_b, f_bias[l], q_norm_g[l], k_norm_g[l]) @ w_out_b[l]
        g_a, g_b = jnp.split(u_g, 2, axis=-1)
        merged = jax.nn.sigmoid(g_a) * y_a + jax.nn.sigmoid(g_b) * y_b
        x = x + merged @ w_out[l]
    return _rmsnorm(x, final_norm_g)
```

```python
import os
from contextlib import ExitStack
import numpy as np
import concourse.bass as bass
import concourse.mybir as mybir
from concourse.bass_utils import run_bass_kernel_spmd

F32 = mybir.dt.float32
BF16 = mybir.dt.bfloat16
AF = mybir.ActivationFunctionType
ALU = mybir.AluOpType
AX = mybir.AxisListType

ENGS = ("pe", "act", "dve", "pool", "sp")
SEM_CH = 12000

T = 2048
D = 1024
NG = 4
C0 = 0.6065306597126334
RMS_EPS = 1e-6
LNX_EPS = 64e-5
IN_COLS = 6280


class Op:
    __slots__ = ("eng", "fn", "deps", "sig", "dsem", "dval", "idx", "cnt")


class Sched:
    def __init__(self):
        self.ops = {e: [] for e in ENGS}
        self.lastw = {}
        self.readers = {}
        self.dreaders = {}
        self.dcount = {}
        self.pending_barrier = {e: [] for e in ENGS}
        self.all_dma = []

    def add(self, eng, fn, reads=(), writes=(), dsem=None):
        op = Op()
        op.eng, op.fn, op.dsem, op.sig = eng, fn, dsem, False
        op.idx = len(self.ops[eng])
        op.cnt = None
        op.dval = None
        is_dma = dsem is not None
        if is_dma:
            self.dcount[dsem] = self.dcount.get(dsem, 0) + 1
            op.dval = 16 * self.dcount[dsem]
        deps = {}
        for r in reads:
            w = self.lastw.get(r)
            if w is not None:
                deps[id(w)] = w
        strict = eng != "pe"
        for wr in writes:
            lw = self.lastw.get(wr)
            if lw is not None and (lw.eng != eng or lw.dsem is not None or is_dma or strict):
                deps[id(lw)] = lw
            for rd in self.readers.get(wr, {}).values():
                if rd.eng != eng or is_dma or strict:
                    deps[id(rd)] = rd
            for rd in self.dreaders.get(wr, ()):
                deps[id(rd)] = rd
        for d in self.pending_barrier[eng]:
            deps[id(d)] = d
        self.pending_barrier[eng] = []
        op.deps = list(deps.values())
        for d in op.deps:
            if d.dsem is None:
                d.sig = True
        for r in reads:
            if is_dma:
                self.dreaders.setdefault(r, []).append(op)
            else:
                self.readers.setdefault(r, {})[eng] = op
        for wr in writes:
            self.lastw[wr] = op
            self.readers[wr] = {}
            self.dreaders[wr] = []
        self.ops[eng].append(op)
        if is_dma:
            self.all_dma.append(op)
        return op

    def barrier(self):
        lasts = []
        for e in ENGS:
            comp = [o for o in self.ops[e] if o.dsem is None]
            if comp:
                lasts.append(comp[-1])
        lastd = {}
        for o in self.all_dma:
            lastd[o.dsem] = o
        lasts.extend(lastd.values())
        for d in lasts:
            if d.dsem is None:
                d.sig = True
        for e in ENGS:
            self.pending_barrier[e] = list(lasts)

    def emit(self, nc, stack):
        for e in ENGS:
            c = 0
            for o in self.ops[e]:
                if o.dsem is None and o.sig:
                    c += 1
                    o.cnt = c
        nsem = {e: (max([o.cnt or 0 for o in self.ops[e]] + [0]) + SEM_CH - 1) // SEM_CH for e in ENGS}
        esems = {e: [stack.enter_context(nc.semaphore(f"s_{e}_{i}")) for i in range(nsem[e])] for e in ENGS}
        dsems = {k: stack.enter_context(nc.semaphore(f"d_{k}")) for k in self.dcount}
        block = stack.enter_context(nc.Block())

        def run(e, eh):
            waited = {}
            for o in self.ops[e]:
                need = {}
                for d in o.deps:
                    if d.dsem is not None:
                        k = ("d", d.dsem)
                        need[k] = max(need.get(k, 0), d.dval)
                    else:
                        k = ("e", d.eng)
                        need[k] = max(need.get(k, 0), d.cnt)
                for k, v in need.items():
                    if waited.get(k, 0) >= v:
                        continue
                    waited[k] = v
                    if k[0] == "d":
                        eh.wait_ge(dsems[k[1]], v)
                    else:
                        blk = (v - 1) // SEM_CH
                        eh.wait_ge(esems[k[1]][blk], v - blk * SEM_CH)
                ins = o.fn(eh)
                if ins is None:
                    continue
                if o.dsem is not None:
                    ins.then_inc(dsems[o.dsem], 16)
                elif o.sig:
                    blk = (o.cnt - 1) // SEM_CH
                    ins.then_inc(esems[e][blk], 1)

        @block.tensor
        def _(eh):
            run("pe", eh)

        @block.scalar
        def _(eh):
            run("act", eh)

        @block.vector
        def _(eh):
            run("dve", eh)

        @block.gpsimd
        def _(eh):
            run("pool", eh)

        @block.sync
        def _(eh):
            run("sp", eh)


class Builder:
    def __init__(self, stage=99, dbg=False):
        self.stage = stage
        self.dbg = dbg
        self.nc = bass.Bass("TRN2", target_bir_lowering=False)
        self.S = Sched()
        self.st = ExitStack()
        self.psn = 0
        self.cur = None
        self.rec = {0: [], 1: []}
        self.ps_range = {0: (0, 4), 1: (4, 4)}
        self.psn_s = {0: 0, 1: 0}
        self.ph = ExitStack()
        self.dbg_outs = {}
        self.uid = 0

    def sb(self, name, shape, dt):
        return self.st.enter_context(self.nc.sbuf_tensor(name, shape, dt))

    def sbp(self, name, shape, dt):
        return self.ph.enter_context(self.nc.sbuf_tensor(name, shape, dt))

    def end_phase(self):
        self.ph.close()
        self.ph = ExitStack()
        self.S.barrier()

    def dram_in(self, name, shape):
        return self.nc.dram_tensor(name, list(shape), F32, kind="ExternalInput").ap()

    def next_ps(self):
        if self.cur is not None:
            lo, n = self.ps_range[self.cur]
            i = lo + self.psn_s[self.cur] % n
            self.psn_s[self.cur] += 1
            return i
        i = self.psn % 8
        self.psn += 1
        return i

    def dump(self, name, ap, reads, shape):
        if not self.dbg:
            return
        dt = ap.dtype
        o = self.nc.dram_tensor("dbg_" + name, list(shape), dt, kind="ExternalOutput").ap()
        self.dbg_outs[name] = (shape, dt)
        self.A("sp", lambda e: e.dma_start(out=o, in_=ap), reads, ["dbgout_" + name], dsem="dbg_" + name)
        self.final_reads.append("dbgout_" + name)

    def A(self, eng, fn, r=(), w=(), dsem=None):
        if self.cur is not None:
            self.rec[self.cur].append((eng, fn, tuple(r), tuple(w), dsem))
            return None
        return self.S.add(eng, fn, reads=r, writes=w, dsem=dsem)

    def mm(self, out, lhsT, rhs, r, w, start=True, stop=True, skip=False):
        if skip:
            self.A("pe", lambda e: e.matmul(out, lhsT=lhsT, rhs=rhs, start=start, stop=stop, skip_group_check=True), r, w)
        else:
            self.A("pe", lambda e: e.matmul(out, lhsT=lhsT, rhs=rhs, start=start, stop=stop), r, w)

    def tr(self, out, in_, ident, r, w):
        self.A("pe", lambda e: e.transpose(out, in_, ident), r, w)

    def build(self):
        nc, S = self.nc, self.S
        self.final_reads = []
        x = self.dram_in("x", [T, D])
        norm_g = self.dram_in("norm_g", [1, D])
        w_in = self.dram_in("w_in", [D, IN_COLS])
        shift_mu = self.dram_in("shift_mu", [1, 2176])
        w_lora_up = self.dram_in("w_lora_up", [64, 512])
        w0 = self.dram_in("w0", [1, 512])
        a_lora_up = self.dram_in("a_lora_up", [64, 512])
        a0 = self.dram_in("a0", [1, 512])
        k_k = self.dram_in("k_k", [1, 512])
        k_a = self.dram_in("k_a", [1, 512])
        r_k = self.dram_in("r_k", [1, 512])
        lnx_w = self.dram_in("lnx_w", [1, 512])
        lnx_b = self.dram_in("lnx_b", [1, 512])
        f_bias = self.dram_in("f_bias", [1, 8])
        q_norm_g = self.dram_in("q_norm_g", [1, 64])
        k_norm_g = self.dram_in("k_norm_g", [1, 64])
        w_out_a = self.dram_in("w_out_a", [512, D])
        w_out_b = self.dram_in("w_out_b", [512, D])
        w_out = self.dram_in("w_out", [D, D])
        final_g = self.dram_in("final_norm_g", [1, D])
        out = nc.dram_tensor("out", [T, D], F32, kind="ExternalOutput").ap()
        self.w_in = w_in

        sb = self.sb
        self.ps = [self.st.enter_context(nc.psum_tensor(f"ps{i}", [128, 512], F32)) for i in range(8)]
        self.psb = [p.bitcast(BF16) for p in self.ps]
        hT = sb("hT", [128, 8, T], BF16)
        self.hT = hT
        ident_f = sb("ident_f", [128, 128], F32)
        ident_b = sb("ident_b", [128, 128], BF16)
        bones = sb("bones", [128, 128], F32)
        M_lt = sb("M_lt", [128, 8, 64], F32)
        M_le = sb("M_le", [128, 8, 64], F32)
        M_gt = sb("M_gt", [128, 8, 64], F32)
        M_fox = sb("M_fox", [128, 128], BF16)
        rst = sb("rst", [128, 512], BF16)
        prm = sb("prm", [128, 96], F32)
        fb_bc = sb("fb_bc", [128, 8], F32)
        fing_bc = sb("fing_bc", [128, D], F32)
        wup = sb("wup", [128, 512], BF16)
        bones_b = sb("bones_b", [128, 128], BF16)
        self.bones_b = bones_b
        self.ident_f, self.ident_b, self.bones = ident_f, ident_b, bones
        self.prm = prm
        MU, OM, W0c, A0c, KKc, KAc, OMKA, LNW, LNB, RKc, QG, KG, NGc = 0, 17, 34, 38, 42, 46, 50, 54, 58, 62, 66, 67, 68

        nparam = [0]

        def pload(dst, src):
            def f(e):
                with nc.allow_non_contiguous_dma(reason="small param load"):
                    return e.dma_start(out=dst, in_=src)
            self.A("sp", f, (), ["params"], dsem="params")

        pload(fb_bc[:], f_bias.partition_broadcast(128))
        pload(fing_bc[:], final_g.partition_broadcast(128))
        prow = sb("prow", [76, 128], F32)
        self.A("pool", lambda e: e.memset(prow[:], 0.0), (), ["prow"])

        def rload(dst, src):
            self.A("sp", lambda e: e.dma_start(out=dst, in_=src), (), ["prow"], dsem="prow")

        rload(prow[MU:MU + 17, :], shift_mu.rearrange("o (c p) -> (o c) p", p=128))
        for col, src in ((W0c, w0), (A0c, a0), (KKc, k_k), (KAc, k_a), (LNW, lnx_w), (LNB, lnx_b), (RKc, r_k)):
            rload(prow[col:col + 4, :], src.rearrange("o (c p) -> (o c) p", p=128))
        for hh in range(2):
            rload(prow[QG:QG + 1, 64 * hh:64 * hh + 64], q_norm_g)
            rload(prow[KG:KG + 1, 64 * hh:64 * hh + 64], k_norm_g)
        rload(prow[NGc:NGc + 8, :], norm_g.rearrange("o (c p) -> (o c) p", p=128))
        self._param_transpose = (prow, prm)
        self.A("pool", lambda e: e.dma_start(out=wup[0:64, :], in_=w_lora_up), (), ["wup"], dsem="wup")
        self.A("pool", lambda e: e.dma_start(out=wup[64:128, :], in_=a_lora_up), (), ["wup"], dsem="wup")
        PR = ["params"]

        self.A("pool", lambda e: e.memset(ident_f[:], 1.0), (), ["ident_f"])
        self.A("pool", lambda e: e.affine_select(out=ident_f[:], in_=ident_f[:], pattern=[[-1, 128]],
                                                  compare_op=ALU.is_equal, fill=0.0, base=0, channel_multiplier=1),
               ["ident_f"], ["ident_f"])
        self.A("dve", lambda e: e.tensor_copy(out=ident_b[:], in_=ident_f[:]), ["ident_f"], ["ident_b"])
        pti = self.next_ps()
        self.tr(self.ps[pti][:, 0:76], prow[0:76, :], ident_f[0:76, 0:76], ["prow", "ident_f"], [f"ps{pti}"])
        self.A("dve", lambda e: e.tensor_copy(out=prm[:, 0:76], in_=self.ps[pti][:, 0:76]), [f"ps{pti}"], ["params"])
        self.A("pool", lambda e: e.memset(bones[:], 0.0), (), ["bones"])
        for hh in range(2):
            self.A("pool", lambda e, hh=hh: e.memset(bones[64 * hh:64 * hh + 64, 64 * hh:64 * hh + 64], 1.0), ["bones"], ["bones"])
        self.A("dve", lambda e: e.tensor_copy(out=bones_b[:], in_=bones[:]), ["bones"], ["bones_b"])
        for (mt, cm, st_, op, nm) in ((M_lt, -1, 1, ALU.is_gt, "M_lt"), (M_le, -1, 1, ALU.is_ge, "M_le"), (M_gt, 1, -1, ALU.is_gt, "M_gt")):
            self.A("pool", lambda e, mt=mt: e.memset(mt[:], 1.0), (), [nm])
            for hh in range(2):
                self.A("pool", lambda e, mt=mt, cm=cm, st_=st_, op=op, hh=hh: e.affine_select(
                    out=mt[64 * hh:64 * hh + 64], in_=mt[64 * hh:64 * hh + 64], pattern=[[0, 8], [st_, 64]],
                    compare_op=op, fill=0.0, base=0, channel_multiplier=cm), [nm], [nm])
        mfox_f = sb("mfox_f", [128, 128], F32)
        self.mfox_f = mfox_f
        self.A("pool", lambda e: e.memset(mfox_f[:], 1.0), (), ["mfox_f"])
        self.A("pool", lambda e: e.affine_select(out=mfox_f[:], in_=mfox_f[:], pattern=[[1, 128]], compare_op=ALU.is_ge,
                                                  fill=0.0, base=0, channel_multiplier=-1), ["mfox_f"], ["mfox_f"])
        self.A("dve", lambda e: e.tensor_copy(out=M_fox[:], in_=mfox_f[:]), ["mfox_f"], ["M_fox"])
        self.A("pool", lambda e: e.memset(rst[:], 1.0), (), ["rst"])
        self.A("pool", lambda e: e.memset(rst[:].rearrange("p (c t) -> p c t", t=64)[:, :, 0:1], 0.0), ["rst"], ["rst"])
        self.A("dve", lambda e: e.tensor_scalar(out=prm[:, OM:OM + 17], in0=prm[:, MU:MU + 17], scalar1=-1.0, scalar2=1.0,
                                                 op0=ALU.mult, op1=ALU.add), PR, ["prm2"])
        self.A("dve", lambda e: e.tensor_scalar(out=prm[:, OMKA:OMKA + 4], in0=prm[:, KAc:KAc + 4], scalar1=-1.0, scalar2=1.0,
                                                 op0=ALU.mult, op1=ALU.add), PR, ["prm2"])
        PR2 = ["params", "prm2"]

        sbp = self.sbp
        xsl = [sbp(f"xsl{i}", [128, D], F32) for i in range(2)]
        xnb = [sbp(f"xnb{i}", [128, D], BF16) for i in range(2)]
        junk = sbp("junk", [128, D], BF16)
        stat = sbp("stat", [128, 64], F32)
        for tt in range(16):
            xs = xsl[tt % 2]
            xk = f"xsl{tt % 2}"
            xn = xnb[tt % 2]
            nk = f"xnb{tt % 2}"
            self.A("sp", lambda e, xs=xs, tt=tt: e.dma_start(out=xs[:], in_=x[tt * 128:(tt + 1) * 128, :]), (), [xk], dsem=xk)
            self.A("act", lambda e, xs=xs, tt=tt: e.activation(out=junk[:], in_=xs[:], func=AF.Square,
                                                                accum_out=stat[:, tt:tt + 1]), [xk], ["junk", f"st{tt}"])
            self.A("dve", lambda e, tt=tt: e.tensor_scalar(out=stat[:, 16 + tt:17 + tt], in0=stat[:, tt:tt + 1], scalar1=1.0 / D,
                                                           scalar2=RMS_EPS, op0=ALU.mult, op1=ALU.add), [f"st{tt}"], [f"st{tt}b"])
            self.A("act", lambda e, tt=tt: e.activation(out=stat[:, 32 + tt:33 + tt], in_=stat[:, 16 + tt:17 + tt], func=AF.Sqrt),
                   [f"st{tt}b"], [f"st{tt}c"])
            self.A("dve", lambda e, tt=tt: e.reciprocal(out=stat[:, 48 + tt:49 + tt], in_=stat[:, 32 + tt:33 + tt]),
                   [f"st{tt}c"], [f"st{tt}d"])
            self.A("dve", lambda e, xs=xs, xn=xn, tt=tt: e.tensor_scalar(out=xn[:], in0=xs[:], scalar1=stat[:, 48 + tt:49 + tt],
                                                                         scalar2=None, op0=ALU.mult), [xk, f"st{tt}d"], [nk])
            pi = self.next_ps()
            for kc in range(8):
                self.tr(self.psb[pi][:, kc * 128:(kc + 1) * 128], xn[:, kc * 128:(kc + 1) * 128], ident_b[:],
                        [nk, "ident_b"], [f"ps{pi}"])
            self.A("dve", lambda e, pi=pi, tt=tt: e.tensor_tensor(
                out=hT[:, :, tt * 128:(tt + 1) * 128],
                in0=self.psb[pi][:, 0:1024].rearrange("p (k t) -> p k t", t=128),
                in1=prm[:, NGc:NGc + 8].unsqueeze(2).to_broadcast([128, 8, 128]), op=ALU.mult),
                [f"ps{pi}"] + PR, [f"hT{tt // 4}"])
        if self.dbg:
            hdump = sbp("hdump", [128, 8, 256], F32)
            self.A("dve", lambda e: e.tensor_copy(out=hdump[:], in_=hT[:, :, 0:256]), ["hT0"], ["hdump"])
            self.dump("hT", hdump[:], ["hdump"], [128, 8, 256])
        self.x, self.out, self.fing_bc, self.fb_bc, self.M_fox = x, out, fing_bc, fb_bc, M_fox
        self.w_out_a, self.w_out_b, self.w_out = w_out_a, w_out_b, w_out
        self.QG, self.KG = QG, KG
        if self.stage <= 1:
            return self.finish(out)
        self.end_phase()

        self.wsl = [sb(f"wsl{i}", [128, 8, 512], BF16) for i in range(2)]
        self.wsn = 0
        self.yaT = sb("yaT", [128, 4, T], BF16)
        self.ones_f = sb("ones_f", [128, 128], F32)
        self.fl = [sb(f"fl{i}", [128, 16, 8], F32) for i in range(5)]
        self.wf = sb("wf", [128, 8, 8], BF16)
        self.PR2 = PR2
        if self.stage == 6:
            self.c0()
        if self.stage != 6:
            self.rwkv(PR2, MU, OM, W0c, A0c, KKc, KAc, OMKA, LNW, LNB, RKc, wup, M_lt, M_le, M_gt, rst)
        if self.stage <= 5:
            return self.finish(out)
        self.end_phase()
        self.ybT = sb("ybT", [128, 4, T], BF16)
        self.woa = sb("woa", [128, 4, D], BF16)
        self.wob = sb("wob", [128, 4, D], BF16)
        self.A("pool", lambda e: e.dma_start(out=self.woa[:], in_=self.w_out_a.rearrange("(k p) c -> p k c", p=128)), (), ["woa"], dsem="woa")
        self.A("pool", lambda e: e.dma_start(out=self.wob[:], in_=self.w_out_b.rearrange("(k p) c -> p k c", p=128)), (), ["wob"], dsem="wob")
        self.fox()
        if self.stage <= 6:
            return self.finish(out)
        self.end_phase()
        self.merge_out()
        return self.finish(out)

    def load_w(self, pieces, slot_i=None):
        if slot_i is None:
            i = self.wsn % 2
            self.wsn += 1
        else:
            i = slot_i
        slot = self.wsl[i]
        off = 0
        for (c0, n) in pieces:
            def f(e, c0=c0, n=n, off=off):
                return e.dma_start(out=slot[:, :, off:off + n],
                                   in_=self.w_in[:, c0:c0 + n].rearrange("(k p) c -> p k c", p=128))
            self.A("pool", f, (), [f"wsl{i}"], dsem=f"wsl{i}")
            off += n
        return slot, f"wsl{i}"

    def inproj_fm(self, slot, skey, off, g, M=128):
        pi = self.next_ps()
        for kc in range(8):
            self.mm(self.ps[pi][0:M, :], slot[:, kc, off:off + M], self.hT[:, kc, g * 512:(g + 1) * 512],
                    [skey, f"hT{g}"], [f"ps{pi}"], start=(kc == 0), stop=(kc == 7))
        return pi

    def rwkv(self, PR2, MU, OM, W0c, A0c, KKc, KAc, OMKA, LNW, LNB, RKc, wup, M_lt, M_le, M_gt, rst):
        nc, sb, prm = self.nc, self.sbp, self.prm
        ps, psb = self.ps, self.psb
        NTS = 4
        tsb = [sb(f"tsb{i}", [128, 514], F32) for i in range(NTS)]
        self.tsn = 0
        self.tsn_s = {0: 0, 1: 0}
        carry = sb("carry", [128, 16], F32)
        self.A("pool", lambda e: e.memset(carry[:], 0.0), (), ["carry"])

        def shift_evac(pi, mu_col, stream, g, dst, dkey, npart=128):
            if self.cur is not None:
                i = self.cur * 2 + self.tsn_s[self.cur] % 2
                self.tsn_s[self.cur] += 1
            else:
                i = self.tsn % NTS
                self.tsn += 1
            ts = tsb[i]
            tk = f"tsb{i}"
            P = slice(0, npart)
            self.A("act", lambda e: e.activation(out=ts[P, 1:513], in_=ps[pi][P, :], func=AF.Copy,
                                                  scale=prm[P, MU + mu_col:MU + mu_col + 1]), [f"ps{pi}"] + PR2, [tk])
            if g == 0:
                self.A("pool", lambda e: e.memset(ts[P, 0:1], 0.0), [tk], [tk])
            else:
                self.A("pool", lambda e: e.tensor_copy(out=ts[P, 0:1], in_=carry[P, stream:stream + 1]), [tk, f"carry{stream}"], [tk])
            self.A("dve", lambda e: e.scalar_tensor_tensor(out=dst, in0=ps[pi][P, :], scalar=prm[P, OM + mu_col:OM + mu_col + 1],
                                                           in1=ts[P, 0:512], op0=ALU.mult, op1=ALU.add),
                   [f"ps{pi}", tk] + PR2, [dkey])
            if g < 3:
                self.A("pool", lambda e: e.tensor_copy(out=carry[P, stream:stream + 1], in_=ts[P, 512:513]), [tk], [f"carry{stream}"])

        twad = sb("twad", [128, T], BF16)
        ltmp = sb("ltmp", [128, 512], F32)
        slot, skey = self.load_w([(1536, 128)], slot_i=1)
        for g in range(NG):
            pi = self.inproj_fm(slot, skey, 0, g)
            shift_evac(pi, 12, 0, g, ltmp[:], "ltmp")
            self.A("act", lambda e, g=g: e.activation(out=twad[0:64, g * 512:(g + 1) * 512], in_=ltmp[0:64, :], func=AF.Tanh),
                   ["ltmp"], [f"twad{g}"])
            self.A("act", lambda e, g=g: e.activation(out=twad[64:128, g * 512:(g + 1) * 512], in_=ltmp[64:128, :], func=AF.Copy),
                   ["ltmp"], [f"twad{g}"])

        self.c0()
        NF, NH = 28, 46
        Fp = [sb(f"F{i}", [128, 512], F32) for i in range(NF)]
        Hp = [sb(f"H{i}", [128, 512], BF16) for i in range(NH)]
        ffree_s = {0: list(range(0, NF // 2)), 1: list(range(NF // 2, NF))}
        hfree_s = {0: list(range(0, NH // 2)), 1: list(range(NH // 2, NH))}

        class Buf:
            pass

        def fa():
            ffree = ffree_s[self.cur]
            i = ffree.pop(0)
            b = Buf()
            b.t, b.k, b.i, b.pool = Fp[i], f"F{i}", i, ffree
            return b

        def ha():
            hfree = hfree_s[self.cur]
            i = hfree.pop(0)
            b = Buf()
            b.t, b.k, b.i, b.pool = Hp[i], f"H{i}", i, hfree
            return b

        def rel(*bs):
            for b in bs:
                b.pool.append(b.i)

        yaT = self.yaT
        Hf_s = [sb(f"Hf{i}", [128, 64], F32) for i in range(2)]
        Hb_s = [sb(f"Hb{i}", [128, 64], BF16) for i in range(2)]
        Xh_s = [sb(f"Xh{i}", [128, 64], F32) for i in range(2)]
        pcs_s = [sb(f"pcs{i}", [128, 32], F32) for i in range(2)]
        gst_s = [sb(f"gst{i}", [128, 64], F32) for i in range(2)]

        def v3(ap, inner):
            return ap.rearrange("p (c t) -> p c t", t=inner)

        def bmm(lh, rh, outcols=64):
            pi = self.next_ps()
            for c in range(8):
                for hh in range(2):
                    P = slice(64 * hh, 64 * hh + 64)
                    self.mm(ps[pi][P, c * 64:(c + 1) * 64], lh.t[P, c * 64:(c + 1) * 64], rh.t[P, c * 64:(c + 1) * 64],
                            [lh.k, rh.k], [f"ps{pi}"])
            return pi

        def btr(src, col0, stride):
            pi = self.next_ps()
            for c in range(8):
                for hh in range(2):
                    P = slice(64 * hh, 64 * hh + 64)
                    self.tr(psb[pi][P, c * 64:(c + 1) * 64], src.t[P, c * stride + col0:c * stride + col0 + 64],
                            self.ident_b[P, P], [src.k, "ident_b"], [f"ps{pi}"])
            return pi


        def batch(hp, g, slot, skey, sidx):
            pcs = pcs_s[sidx][:, g * 8:(g + 1) * 8]
            kpcs = f"pcs{sidx}_{g}"
            if True:
                gs = slice(g * 512, (g + 1) * 512)
                Fr, Fk, Fv, Fg = fa(), fa(), fa(), fa()
                for j, (dst, mc) in enumerate(((Fr, hp), (Fk, 4 + hp), (Fv, 8 + hp), (Fg, 13 + hp))):
                    pi = self.inproj_fm(slot, skey, j * 128, g)
                    shift_evac(pi, mc, 1 + sidx * 4 + j, g, dst.t[:], dst.k)
                    yield
                Fs, Fa_ = fa(), fa()
                pi = self.next_ps()
                self.mm(ps[pi][:, :], wup[0:64, hp * 128:(hp + 1) * 128], twad[0:64, gs], ["wup", f"twad{g}"], [f"ps{pi}"])
                self.A("act", lambda e, pi=pi, Fs=Fs: e.activation(out=Fs.t[:], in_=ps[pi][:, :], func=AF.Sigmoid,
                                                                     bias=prm[:, W0c + hp:W0c + hp + 1]), [f"ps{pi}"] + PR2, [Fs.k])
                pi = self.next_ps()
                self.mm(ps[pi][:, :], wup[64:128, hp * 128:(hp + 1) * 128], twad[64:128, gs], ["wup", f"twad{g}"], [f"ps{pi}"])
                self.A("act", lambda e, pi=pi, Fa_=Fa_: e.activation(out=Fa_.t[:], in_=ps[pi][:, :], func=AF.Sigmoid,
                                                                       bias=prm[:, A0c + hp:A0c + hp + 1]), [f"ps{pi}"] + PR2, [Fa_.k])
                if self.dbg and hp == 0 and g == 1:
                    self.dump("r_mixed", Fr.t[:], [Fr.k], [128, 512])
                    self.dump("k_mixed", Fk.t[:], [Fk.k], [128, 512])
                    self.dump("sig", Fs.t[:], [Fs.k], [128, 512])
                    self.dump("a", Fa_.t[:], [Fa_.k], [128, 512])
                yield
                Fkk, Ft1 = fa(), fa()
                self.A("dve", lambda e, Fkk=Fkk, Fk=Fk: e.tensor_scalar(out=Fkk.t[:], in0=Fk.t[:], scalar1=prm[:, KKc + hp:KKc + hp + 1],
                                                                         scalar2=None, op0=ALU.mult), [Fk.k] + PR2, [Fkk.k])
                Hsq = ha()
                self.A("act", lambda e, Hsq=Hsq, Fkk=Fkk: e.activation(out=Hsq.t[:], in_=Fkk.t[:], func=AF.Square), [Fkk.k], [Hsq.k])
                pi = self.next_ps()
                self.mm(ps[pi][:, :], self.bones_b[:], Hsq.t[:], ["bones_b", Hsq.k], [f"ps{pi}"])
                rel(Hsq)
                self.A("act", lambda e, pi=pi, Ft1=Ft1: e.activation(out=Ft1.t[:], in_=ps[pi][:, :], func=AF.Sqrt), [f"ps{pi}"], [Ft1.k])
                self.A("dve", lambda e, Ft1=Ft1: e.tensor_scalar_max(out=Ft1.t[:], in0=Ft1.t[:], scalar1=1e-12), [Ft1.k], [Ft1.k])
                self.A("dve", lambda e, Ft1=Ft1: e.reciprocal(out=Ft1.t[:], in_=Ft1.t[:]), [Ft1.k], [Ft1.k])
                self.A("dve", lambda e, Fkk=Fkk, Ft1=Ft1: e.tensor_tensor(out=Fkk.t[:], in0=Fkk.t[:], in1=Ft1.t[:], op=ALU.mult),
                       [Fkk.k, Ft1.k], [Fkk.k])
                if self.dbg and hp == 0 and g == 1:
                    self.dump("kkn", Fkk.t[:], [Fkk.k], [128, 512])
                self.A("dve", lambda e, Ft1=Ft1, Fa_=Fa_: e.tensor_scalar(out=Ft1.t[:], in0=Fa_.t[:], scalar1=prm[:, KAc + hp:KAc + hp + 1],
                                                                            scalar2=prm[:, OMKA + hp:OMKA + hp + 1], op0=ALU.mult, op1=ALU.add),
                       [Fa_.k] + PR2, [Ft1.k])
                self.A("dve", lambda e, Fk=Fk, Ft1=Ft1: e.tensor_tensor(out=Fk.t[:], in0=Fk.t[:], in1=Ft1.t[:], op=ALU.mult),
                       [Fk.k, Ft1.k], [Fk.k])
                self.A("dve", lambda e, Fa_=Fa_, Fkk=Fkk: e.tensor_tensor(out=Fa_.t[:], in0=Fa_.t[:], in1=Fkk.t[:], op=ALU.mult),
                       [Fa_.k, Fkk.k], [Fa_.k])
                yield
                Hv = ha()
                self.A("act", lambda e, Hv=Hv, Fv=Fv: e.activation(out=Hv.t[:], in_=Fv.t[:], func=AF.Copy), [Fv.k], [Hv.k])
                Hrk = ha()
                self.A("dve", lambda e, Hrk=Hrk, Fr=Fr, Fk=Fk: e.scalar_tensor_tensor(out=Hrk.t[:], in0=Fr.t[:], scalar=prm[:, RKc + hp:RKc + hp + 1],
                                                                                     in1=Fk.t[:], op0=ALU.mult, op1=ALU.mult),
                       [Fr.k, Fk.k] + PR2, [Hrk.k])
                pi = self.next_ps()
                self.mm(ps[pi][:, :], self.bones_b[:], Hrk.t[:], ["bones_b", Hrk.k], [f"ps{pi}"])
                rel(Hrk)
                self.A("dve", lambda e, pi=pi, Fv=Fv: e.tensor_tensor(out=Fv.t[:], in0=ps[pi][:, :], in1=Fv.t[:], op=ALU.mult),
                       [f"ps{pi}", Fv.k, Hv.k], [Fv.k])
                self.A("act", lambda e, Fg=Fg: e.activation(out=Fg.t[:], in_=Fg.t[:], func=AF.Silu), [Fg.k], [Fg.k])
                self.A("dve", lambda e, Ft1=Ft1, Fs=Fs: e.tensor_tensor_scan(out=Ft1.t[:], data0=rst[:], data1=Fs.t[:], initial=0.0,
                                                                              op0=ALU.mult, op1=ALU.add), ["rst", Fs.k], [Ft1.k])
                if self.dbg and hp == 0 and g == 1:
                    self.dump("Ls", Ft1.t[:], [Ft1.k], [128, 512])
                    self.dump("k2", Fk.t[:], [Fk.k], [128, 512])
                Fe = fa()
                Hrt, Hat, Hbt, Hkt, Hbh, Hkh = ha(), ha(), ha(), ha(), ha(), ha()
                self.A("act", lambda e, Fe=Fe, Ft1=Ft1: e.activation(out=Fe.t[:], in_=Ft1.t[:], func=AF.Exp, scale=-C0), [Ft1.k], [Fe.k])
                self.A("dve", lambda e, Hrt=Hrt, Fr=Fr, Fe=Fe: e.tensor_tensor(out=Hrt.t[:], in0=Fr.t[:], in1=Fe.t[:], op=ALU.mult),
                       [Fr.k, Fe.k], [Hrt.k])
                self.A("dve", lambda e, Fe=Fe: e.tensor_copy(out=pcs[:, :], in_=v3(Fe.t[:], 64)[:, :, 63]), [Fe.k], [kpcs])
                yield
                self.A("act", lambda e, Fr=Fr, Ft1=Ft1: e.activation(out=Fr.t[:], in_=Ft1.t[:], func=AF.Exp, scale=C0), [Ft1.k, Hrt.k], [Fr.k])
                self.A("dve", lambda e, Hbt=Hbt, Fa_=Fa_, Fr=Fr: e.tensor_tensor(out=Hbt.t[:], in0=Fa_.t[:], in1=Fr.t[:], op=ALU.mult),
                       [Fa_.k, Fr.k], [Hbt.k])
                self.A("dve", lambda e, Hkt=Hkt, Fk=Fk, Fr=Fr: e.tensor_tensor(out=Hkt.t[:], in0=Fk.t[:], in1=Fr.t[:], op=ALU.mult),
                       [Fk.k, Fr.k], [Hkt.k])
                self.A("dve", lambda e, Fs=Fs, Ft1=Ft1: e.tensor_tensor(out=Fs.t[:], in0=Ft1.t[:], in1=Fs.t[:], op=ALU.subtract),
                       [Ft1.k, Fs.k], [Fs.k])
                self.A("act", lambda e, Fe=Fe, Fs=Fs: e.activation(out=Fe.t[:], in_=Fs.t[:], func=AF.Exp, scale=-C0), [Fs.k, Hrt.k, kpcs], [Fe.k])
                self.A("dve", lambda e, Hat=Hat, Fkk=Fkk, Fe=Fe: e.scalar_tensor_tensor(out=Hat.t[:], in0=Fkk.t[:], scalar=-1.0, in1=Fe.t[:],
                                                                                        op0=ALU.mult, op1=ALU.mult), [Fkk.k, Fe.k], [Hat.k])
                yield
                self.A("dve", lambda e, Fs=Fs, Ft1=Ft1: e.tensor_tensor(out=v3(Fs.t[:], 64), in0=v3(Ft1.t[:], 64)[:, :, 63:64].to_broadcast([128, 8, 64]),
                                                                        in1=v3(Ft1.t[:], 64), op=ALU.subtract), [Ft1.k, Fs.k], [Fs.k])
                self.A("act", lambda e, Fe=Fe, Fs=Fs: e.activation(out=Fe.t[:], in_=Fs.t[:], func=AF.Exp, scale=-C0), [Fs.k, Hat.k], [Fe.k])
                self.A("dve", lambda e, Hbh=Hbh, Fa_=Fa_, Fe=Fe: e.tensor_tensor(out=Hbh.t[:], in0=Fa_.t[:], in1=Fe.t[:], op=ALU.mult),
                       [Fa_.k, Fe.k], [Hbh.k])
                self.A("dve", lambda e, Hkh=Hkh, Fk=Fk, Fe=Fe: e.tensor_tensor(out=Hkh.t[:], in0=Fk.t[:], in1=Fe.t[:], op=ALU.mult),
                       [Fk.k, Fe.k], [Hkh.k])
                if self.dbg and hp == 0 and g == 1:
                    for nm, b in (("rt", Hrt), ("at", Hat), ("bt", Hbt), ("kt", Hkt), ("bh", Hbh), ("kh", Hkh), ("vb", Hv)):
                        self.dump(nm, b.t[:], [b.k], [128, 512])
                    self.dump("bonus", Fv.t[:], [Fv.k], [128, 512])
                    self.dump("sg", Fg.t[:], [Fg.k], [128, 512])
                rel(Fr, Fk, Fs, Fa_, Fkk, Ft1, Fe)
                if self.stage <= 2:
                    rel(Fv, Fg, Hrt, Hat, Hbt, Hkt, Hbh, Hkh, Hv)
                    return

                yield
                N0T, N0, LrbT, LrkT = ha(), ha(), ha(), ha()
                pi = bmm(Hbt, Hat)
                self.A("dve", lambda e, pi=pi, N0T=N0T: e.tensor_tensor(out=N0T.t[:], in0=ps[pi][:, :], in1=M_lt[:].rearrange("p c t -> p (c t)"), op=ALU.mult),
                       [f"ps{pi}", "M_lt"], [N0T.k])
                pi = bmm(Hat, Hbt)
                self.A("dve", lambda e, pi=pi, N0=N0: e.tensor_tensor(out=N0.t[:], in0=ps[pi][:, :], in1=M_gt[:].rearrange("p c t -> p (c t)"), op=ALU.mult),
                       [f"ps{pi}", "M_gt"], [N0.k])
                pi = bmm(Hbt, Hrt)
                self.A("dve", lambda e, pi=pi, LrbT=LrbT: e.tensor_tensor(out=LrbT.t[:], in0=ps[pi][:, :], in1=M_le[:].rearrange("p c t -> p (c t)"), op=ALU.mult),
                       [f"ps{pi}", "M_le"], [LrbT.k])
                pi = bmm(Hkt, Hrt)
                self.A("dve", lambda e, pi=pi, LrkT=LrkT: e.tensor_tensor(out=LrkT.t[:], in0=ps[pi][:, :], in1=M_le[:].rearrange("p c t -> p (c t)"), op=ALU.mult),
                       [f"ps{pi}", "M_le"], [LrkT.k])
                Yb = [ha(), ha()]

                def yv(b):
                    return b.t[:].rearrange("p (c n) -> p c n", n=128)

                pi = btr(Hat, 0, 64)
                for half in range(2):
                    self.A("act", lambda e, pi=pi, half=half, d=Yb[half]: e.activation(out=yv(d)[:, :, 0:64],
                                                                            in_=v3(psb[pi][:, 0:512], 64)[:, half * 4:half * 4 + 4, :], func=AF.Copy),
                           [f"ps{pi}"], [Yb[half].k])
                pi = bmm(Hat, Hkt)
                for half in range(2):
                    self.A("dve", lambda e, pi=pi, half=half, d=Yb[half]: e.tensor_tensor(out=yv(d)[:, :, 64:128],
                                                                              in0=v3(ps[pi][:, :], 64)[:, half * 4:half * 4 + 4, :],
                                                                              in1=M_gt[:, half * 4:half * 4 + 4, :], op=ALU.mult),
                           [f"ps{pi}", "M_gt"], [Yb[half].k])
                yield
                Btk, Ktk, Vtk = ha(), ha(), ha()
                for ii, (src, dst) in enumerate(((Hbh, Btk), (Hkh, Ktk), (Hv, Vtk))):
                    pi = btr(src, 0, 64)
                    if ii == 1:
                        self.A("dve", lambda e, pi=pi, dst=dst: e.tensor_copy(out=dst.t[:], in_=psb[pi][:, 0:512]), [f"ps{pi}"], [dst.k])
                    else:
                        self.A("act", lambda e, pi=pi, dst=dst: e.activation(out=dst.t[:], in_=psb[pi][:, 0:512], func=AF.Copy), [f"ps{pi}"], [dst.k])
                rel(Hbh, Hkh, Hv, Hbt, Hkt)
                yield
                NkT, Nk = N0T, N0
                for lev in range(6):
                    NT2, N2 = None, None
                    if lev < 5:
                        NT2 = ha()
                        pi = bmm(Nk, NkT)
                        self.A("act", lambda e, pi=pi, NT2=NT2: e.activation(out=NT2.t[:], in_=ps[pi][:, :], func=AF.Copy), [f"ps{pi}"], [NT2.k])
                        if lev < 4:
                            N2 = ha()
                            pi = bmm(NkT, Nk)
                            self.A("dve", lambda e, pi=pi, N2=N2: e.tensor_copy(out=N2.t[:], in_=ps[pi][:, :]), [f"ps{pi}"], [N2.k])
                        yield
                    Yn = [ha(), ha()]
                    for half in range(2):
                        pi = self.next_ps()
                        for c4 in range(4):
                            c = half * 4 + c4
                            for hh in range(2):
                                P = slice(64 * hh, 64 * hh + 64)
                                if half == 0:
                                    self.mm(ps[pi][P, c4 * 128:(c4 + 1) * 128], self.ident_b[P, P], Yb[half].t[P, c4 * 128:(c4 + 1) * 128],
                                            ["ident_b", Yb[half].k], [f"ps{pi}"], start=True, stop=False)
                                self.mm(ps[pi][P, c4 * 128:(c4 + 1) * 128], NkT.t[P, c * 64:(c + 1) * 64], Yb[half].t[P, c4 * 128:(c4 + 1) * 128],
                                        [NkT.k, Yb[half].k], [f"ps{pi}"], start=(half == 1), stop=True)
                        if half == 0:
                            self.A("act", lambda e, pi=pi, d=Yn[half]: e.activation(out=d.t[:], in_=ps[pi][:, :], func=AF.Copy), [f"ps{pi}"], [Yn[half].k])
                        else:
                            self.A("dve", lambda e, pi=pi, d=Yn[half], o=Yb[half]: e.tensor_tensor(out=d.t[:], in0=ps[pi][:, :], in1=o.t[:], op=ALU.add),
                                   [f"ps{pi}", Yb[half].k], [Yn[half].k])
                    rel(Yb[0], Yb[1])
                    Yb = Yn
                    if lev < 5:
                        rel(NkT)
                        if Nk is not None:
                            rel(Nk)
                        NkT, Nk = NT2, N2
                    yield
                rel(NkT)
                def ybmm(col0, rh):
                    pi = self.next_ps()
                    for c in range(8):
                        half, c4 = c // 4, c % 4
                        for hh in range(2):
                            P = slice(64 * hh, 64 * hh + 64)
                            self.mm(ps[pi][P, c * 64:(c + 1) * 64], Yb[half].t[P, c4 * 128 + col0:c4 * 128 + col0 + 64], rh.t[P, c * 64:(c + 1) * 64],
                                    [Yb[half].k, rh.k], [f"ps{pi}"])
                    return pi

                GT, QT, RyT, QyT = ha(), ha(), ha(), ha()
                pi = ybmm(0, Btk)
                self.A("act", lambda e, pi=pi: e.activation(out=GT.t[:], in_=ps[pi][:, :], func=AF.Copy), [f"ps{pi}"], [GT.k])
                pi = ybmm(64, Btk)
                self.A("dve", lambda e, pi=pi: e.tensor_tensor(out=QT.t[:], in0=ps[pi][:, :], in1=Ktk.t[:], op=ALU.add), [f"ps{pi}", Ktk.k], [QT.k])
                yield
                pi = ybmm(0, LrbT)
                self.A("dve", lambda e, pi=pi: e.tensor_tensor(out=RyT.t[:], in0=ps[pi][:, :], in1=Hrt.t[:], op=ALU.add), [f"ps{pi}", Hrt.k], [RyT.k])
                pi = ybmm(64, LrbT)
                self.A("dve", lambda e, pi=pi: e.tensor_tensor(out=QyT.t[:], in0=ps[pi][:, :], in1=LrkT.t[:], op=ALU.add), [f"ps{pi}", LrkT.k], [QyT.k])
                rel(Yb[0], Yb[1], Hat, Hrt, LrbT, LrkT, Btk, Ktk)
                yield
                if self.dbg and hp == 0 and g == 1:
                    for nm, b in (("GT", GT), ("QT", QT), ("RyT", RyT), ("QyT", QyT), ("Vtk", Vtk)):
                        self.dump(nm, b.t[:], [b.k], [128, 512])
                if self.stage <= 3:
                    rel(Fv, Fg, GT, QT, RyT, QyT, Vtk)
                    return

                return dict(GT=GT, QT=QT, RyT=RyT, QyT=QyT, Vtk=Vtk, Fv=Fv, Fg=Fg)

        def seq(hp, g, sidx, R):
            Hf, Hb, gst = Hf_s[sidx], Hb_s[sidx], gst_s[sidx]
            kHf, kHb, kgst = f"Hf{sidx}", f"Hb{sidx}", f"gst{sidx}"
            pcs = pcs_s[sidx][:, g * 8:(g + 1) * 8]
            kpcs = f"pcs{sidx}_{g}"
            GT, QT, RyT, QyT, Vtk, Fv, Fg = R["GT"], R["QT"], R["RyT"], R["QyT"], R["Vtk"], R["Fv"], R["Fg"]
            if True:
                gs = slice(g * 512, (g + 1) * 512)
                Yraw = fa()
                for c in range(8):
                    cs = slice(c * 64, (c + 1) * 64)
                    py, ph = self.next_ps(), self.next_ps()
                    for hh in range(2):
                        P = slice(64 * hh, 64 * hh + 64)
                        self.mm(ps[ph][P, 0:64], QT.t[P, cs], Vtk.t[P, cs], [QT.k, Vtk.k], [f"ps{ph}"], start=True, stop=False)
                        self.mm(ps[ph][P, 0:64], GT.t[P, cs], Hb[P, :], [GT.k, kHb], [f"ps{ph}"], start=False, stop=True)
                    for hh in range(2):
                        P = slice(64 * hh, 64 * hh + 64)
                        self.mm(ps[py][P, 0:64], QyT.t[P, cs], Vtk.t[P, cs], [QyT.k, Vtk.k], [f"ps{py}"], start=True, stop=False)
                        self.mm(ps[py][P, 0:64], RyT.t[P, cs], Hb[P, :], [RyT.k, kHb], [f"ps{py}"], start=False, stop=True)
                    self.A("dve", lambda e, ph=ph, c=c: e.scalar_tensor_tensor(out=Hb[:], in0=Hf[:], scalar=pcs[:, c:c + 1], in1=ps[ph][:, 0:64],
                                                                              op0=ALU.mult, op1=ALU.add), [f"ps{ph}", kHf, kpcs], [kHb])
                    self.A("dve", lambda e, ph=ph, c=c: e.scalar_tensor_tensor(out=Hf[:], in0=Hf[:], scalar=pcs[:, c:c + 1], in1=ps[ph][:, 0:64],
                                                                              op0=ALU.mult, op1=ALU.add), [f"ps{ph}", kHf, kpcs], [kHf])
                    self.A("act", lambda e, py=py, Yraw=Yraw, cs=cs: e.activation(out=Yraw.t[:, cs], in_=ps[py][:, 0:64], func=AF.Copy),
                           [f"ps{py}"], [Yraw.k])
                    yield
                rel(GT, QT, RyT, QyT, Vtk)
                if self.dbg and hp == 0 and g == 1:
                    self.dump("Yraw", Yraw.t[:], [Yraw.k], [128, 512])
                Fsq = fa()
                self.A("act", lambda e, Fsq=Fsq, Yraw=Yraw: e.activation(out=Fsq.t[:], in_=Yraw.t[:], func=AF.Square), [Yraw.k], [Fsq.k])
                self.A("dve", lambda e, Yraw=Yraw: e.tensor_reduce(out=gst[:, 0:8], in_=v3(Yraw.t[:], 64), axis=AX.X, op=ALU.add), [Yraw.k], [kgst])
                self.A("dve", lambda e, Fsq=Fsq: e.tensor_reduce(out=gst[:, 8:16], in_=v3(Fsq.t[:], 64), axis=AX.X, op=ALU.add), [Fsq.k, kgst], [kgst])
                self.A("dve", lambda e: e.tensor_scalar(out=gst[:, 16:24], in0=gst[:, 0:8], scalar1=1.0 / 64, scalar2=None, op0=ALU.mult), [kgst], [kgst])
                self.A("dve", lambda e: e.tensor_tensor(out=gst[:, 24:32], in0=gst[:, 16:24], in1=gst[:, 16:24], op=ALU.mult), [kgst], [kgst])
                self.A("dve", lambda e: e.scalar_tensor_tensor(out=gst[:, 32:40], in0=gst[:, 8:16], scalar=1.0 / 64, in1=gst[:, 24:32],
                                                                op0=ALU.mult, op1=ALU.subtract), [kgst], [kgst])
                self.A("dve", lambda e: e.tensor_scalar(out=gst[:, 32:40], in0=gst[:, 32:40], scalar1=LNX_EPS, scalar2=None, op0=ALU.add), [kgst], [kgst])
                self.A("act", lambda e: e.activation(out=gst[:, 40:48], in_=gst[:, 32:40], func=AF.Sqrt), [kgst], [kgst])
                self.A("dve", lambda e: e.reciprocal(out=gst[:, 48:56], in_=gst[:, 40:48]), [kgst], [kgst])
                self.A("dve", lambda e, Yraw=Yraw: e.tensor_tensor(out=v3(Yraw.t[:], 64), in0=v3(Yraw.t[:], 64),
                                                                   in1=gst[:, 16:24].unsqueeze(2).to_broadcast([128, 8, 64]), op=ALU.subtract),
                       [Yraw.k, kgst], [Yraw.k])
                Ynb = ha()
                self.A("dve", lambda e, Yraw=Yraw, Ynb=Ynb: e.tensor_tensor(out=v3(Ynb.t[:], 64), in0=v3(Yraw.t[:], 64),
                                                                            in1=gst[:, 48:56].unsqueeze(2).to_broadcast([128, 8, 64]), op=ALU.mult),
                       [Yraw.k, kgst], [Ynb.k])
                pi = btr(Ynb, 0, 64)
                self.A("act", lambda e, pi=pi, Fsq=Fsq: e.activation(out=Fsq.t[:], in_=psb[pi][:, 0:512], func=AF.Identity,
                                                                       scale=prm[:, LNW + hp:LNW + hp + 1], bias=prm[:, LNB + hp:LNB + hp + 1]),
                       [f"ps{pi}"] + PR2, [Fsq.k])
                self.A("dve", lambda e, Fsq=Fsq, Fv=Fv: e.tensor_tensor(out=Fsq.t[:], in0=Fsq.t[:], in1=Fv.t[:], op=ALU.add), [Fsq.k, Fv.k], [Fsq.k])
                self.A("dve", lambda e, Fsq=Fsq, Fg=Fg, gs=gs: e.tensor_tensor(out=yaT[:, hp, gs], in0=Fsq.t[:], in1=Fg.t[:], op=ALU.mult),
                       [Fsq.k, Fg.k], [f"yaT{g}"])
                rel(Yraw, Fsq, Ynb, Fv, Fg)
        def stream(hp, sidx):
            Hf, Hb = Hf_s[sidx], Hb_s[sidx]
            slot, skey = self.load_w([(hp * 128, 128), (512 + hp * 128, 128), (1024 + hp * 128, 128), (1664 + hp * 128, 128)], slot_i=sidx)
            self.A("pool", lambda e: e.memset(Hf[:], 0.0), (), [f"Hf{sidx}"])
            self.A("pool", lambda e: e.memset(Hb[:], 0.0), (), [f"Hb{sidx}"])
            R_prev = yield from batch(hp, 0, slot, skey, sidx)
            for g in range(1, NG + 1):
                if R_prev is None:
                    return
                gens = []
                res = {}
                sg_ = seq(hp, g - 1, sidx, R_prev)
                bg_ = batch(hp, g, slot, skey, sidx) if g < NG else None
                alive = [sg_, bg_] if bg_ is not None else [sg_]
                NSQ = int(os.environ.get("KNSQ", "1"))
                NBT = int(os.environ.get("KNBT", "1"))
                while alive:
                    for gen in list(alive):
                        for _rep in range(NSQ if gen is sg_ else NBT):
                            if gen not in alive:
                                break
                            try:
                                next(gen)
                                yield
                            except StopIteration as ex:
                                alive.remove(gen)
                                if gen is bg_:
                                    res["R"] = ex.value
                R_prev = res.get("R")

        self.cur = None
        ffree_s[None] = list(range(NF))
        hfree_s[None] = list(range(NH))
        for pair in range(2):
            gens = [stream(2 * pair, 0), stream(2 * pair + 1, 1)]
            STAG = int(os.environ.get("KSTAG", "12"))
            alive = [True, True]
            for _ in range(STAG):
                try:
                    next(gens[0])
                except StopIteration:
                    alive[0] = False
                    break
            while any(alive):
                for i in range(2):
                    if alive[i]:
                        try:
                            next(gens[i])
                        except StopIteration:
                            alive[i] = False
        self.q_pre = self.load_w([(2176, 512)], slot_i=0)
        self.k_pre = self.load_w([(2688, 512)], slot_i=1)
        if self.dbg:
            yd = sb("yd", [128, 4, 512], F32)
            self.A("dve", lambda e: e.tensor_copy(out=yd[:], in_=yaT[:, :, 512:1024]), [f"yaT{g}" for g in range(4)], ["yd"])
            self.dump("yaT", yd[:], ["yd"], [128, 4, 512])


    def c0(self):
        nc, prm, ps, psb, hT = self.nc, self.prm, self.ps, self.psb, self.hT
        fl, ones_f = self.fl, self.ones_f
        fl2 = lambda t: t[:].rearrange("p a b -> p (a b)")
        self.A("pool", lambda e: e.memset(ones_f[:], 1.0), (), ["ones_f"])
        slot, skey = self.wf, "wf"
        self.A("pool", lambda e: e.dma_start(out=self.wf[:, :, :], in_=self.w_in[:, 4224:4232].rearrange("(k p) c -> p k c", p=128)), (), ["wf"], dsem="wf")
        pi = self.next_ps()
        for tt in range(16):
            for kc in range(8):
                self.mm(ps[pi][:, tt * 8:(tt + 1) * 8], hT[:, kc, tt * 128:(tt + 1) * 128], slot[:, kc, 0:8],
                        [skey, f"hT{tt // 4}"], [f"ps{pi}"], start=(kc == 0), stop=(kc == 7))
        self.A("dve", lambda e, pi=pi: e.tensor_tensor(out=fl[0][:], in0=ps[pi][:, 0:128].rearrange("p (a b) -> p a b", b=8),
                                                        in1=self.fb_bc[:, :].unsqueeze(1).to_broadcast([128, 16, 8]), op=ALU.add),
               [f"ps{pi}", "params"], ["fl0"])
        self.A("act", lambda e: e.activation(out=fl2(fl[0]), in_=fl2(fl[0]), func=AF.Exp, scale=-1.0), ["fl0"], ["fl0"])
        self.A("dve", lambda e: e.tensor_scalar(out=fl2(fl[0]), in0=fl2(fl[0]), scalar1=1.0, scalar2=None, op0=ALU.add), ["fl0"], ["fl0"])
        self.A("act", lambda e: e.activation(out=fl2(fl[0]), in_=fl2(fl[0]), func=AF.Ln), ["fl0"], ["fl0"])
        ploc, ptot = self.next_ps(), self.next_ps()
        self.mm(ps[ploc][:, 0:128], self.mfox_f[:], fl2(fl[0]), ["mfox_f", "fl0"], [f"ps{ploc}"])
        self.mm(ps[ptot][:, 0:128], ones_f[:], fl2(fl[0]), ["ones_f", "fl0"], [f"ps{ptot}"])
        self.A("act", lambda e: e.activation(out=fl2(fl[1]), in_=ps[ptot][:, 0:128], func=AF.Copy), [f"ps{ptot}"], ["fl1"])
        self.A("dve", lambda e: e.tensor_copy(out=fl[2][:, 0, :], in_=fl[1][:, 0, :]), ["fl1"], ["fl2"])
        for tt in range(1, 16):
            self.A("dve", lambda e, tt=tt: e.tensor_tensor(out=fl[2][:, tt, :], in0=fl[2][:, tt - 1, :], in1=fl[1][:, tt, :], op=ALU.add),
                   ["fl1", "fl2"], ["fl2"])
        self.A("dve", lambda e: e.tensor_tensor(out=fl2(fl[3]), in0=fl2(fl[2]), in1=fl2(fl[1]), op=ALU.subtract), ["fl1", "fl2"], ["fl3"])
        self.A("dve", lambda e: e.tensor_tensor(out=fl2(fl[4]), in0=ps[ploc][:, 0:128], in1=fl2(fl[3]), op=ALU.add), [f"ps{ploc}", "fl3"], ["fl4"])
        if self.dbg:
            self.dump("Ccum", fl[4][:], ["fl4"], [128, 16, 8])

    def fox(self):
        nc, sbp, prm, ps, psb, hT = self.nc, self.sbp, self.prm, self.ps, self.psb, self.hT
        PR2 = self.PR2
        QG, KG = self.QG, self.KG
        qT = sbp("qT", [128, 4, T], BF16)
        kT = sbp("kT", [128, 4, T], BF16)
        Vx = sbp("Vx", [128, 16, 8, 65], BF16)
        sgB = sbp("sgB", [128, 4, T], BF16)
        ones_f, fl = self.ones_f, self.fl
        ft = [sbp(f"ft{i}", [128, 512], F32) for i in range(4)]
        fh = [sbp(f"fh{i}", [128, 512], BF16) for i in range(2)]
        self.A("pool", lambda e: e.memset(Vx[:, :, :, 64:65], 1.0), (), ["Vx1"])
        fl2 = lambda t: t[:].rearrange("p a b -> p (a b)")

        nft = [0]

        def ftmp():
            i = nft[0] % 4
            nft[0] += 1
            return ft[i], f"ft{i}"

        def qk_group(dst, dname, gcol, sc, epsv, slot, skey, hp, g):
            gs = slice(g * 512, (g + 1) * 512)
            pi = self.inproj_fm(slot, skey, hp * 128, g)
            fsq, ksq = ftmp()
            fraw, kraw = ftmp()
            hq = fh[(nft[0] // 2) % 2]
            khq = f"fh{(nft[0] // 2) % 2}"
            self.A("act", lambda e: e.activation(out=hq[:], in_=ps[pi][:, :], func=AF.Square), [f"ps{pi}"], [khq])
            self.A("act", lambda e: e.activation(out=fraw[:], in_=ps[pi][:, :], func=AF.Copy, scale=prm[:, gcol:gcol + 1]), [f"ps{pi}"] + PR2, [kraw])
            p2 = self.next_ps()
            self.mm(ps[p2][:, :], self.bones_b[:], hq[:], ["bones_b", khq], [f"ps{p2}"])
            self.A("dve", lambda e: e.tensor_scalar(out=fsq[:], in0=ps[p2][:, :], scalar1=sc, scalar2=epsv, op0=ALU.mult, op1=ALU.add),
                   [f"ps{p2}"], [ksq])
            self.A("act", lambda e: e.activation(out=fsq[:], in_=fsq[:], func=AF.Sqrt), [ksq], [ksq])
            self.A("dve", lambda e: e.reciprocal(out=fsq[:], in_=fsq[:]), [ksq], [ksq])
            self.A("pool", lambda e: e.tensor_tensor(out=dst[:, hp, gs], in0=fraw[:], in1=fsq[:], op=ALU.mult), [kraw, ksq], [f"{dname}{hp}_{g}"])

        slot_q = self.q_pre if hasattr(self, "q_pre") else self.load_w([(2176, 512)], slot_i=0)
        slot_k = self.k_pre if hasattr(self, "k_pre") else self.load_w([(2688, 512)], slot_i=1)
        for hp in range(4):
            for g in range(NG):
                qk_group(qT, "qT", QG, 1.0, 64 * RMS_EPS, slot_q[0], slot_q[1], hp, g)
        slot_v = self.load_w([(3200, 512)], slot_i=0)
        for hp in range(4):
            for g in range(NG):
                qk_group(kT, "kT", KG, 1.0 / 64, RMS_EPS, slot_k[0], slot_k[1], hp, g)
        slot_g = self.load_w([(3712, 512)], slot_i=1)
        slot, skey = slot_v
        for tt in range(16):
            pi = self.next_ps()
            for kc in range(8):
                self.mm(ps[pi][:, :], hT[:, kc, tt * 128:(tt + 1) * 128], slot[:, kc, 0:512], [skey, f"hT{tt // 4}"], [f"ps{pi}"],
                        start=(kc == 0), stop=(kc == 7))
            eng = "act" if tt % 2 == 0 else "dve"
            if eng == "act":
                self.A("act", lambda e, pi=pi, tt=tt: e.activation(out=Vx[:, tt, :, 0:64], in_=ps[pi][:, :].rearrange("p (h d) -> p h d", d=64),
                                                                    func=AF.Copy), [f"ps{pi}"], [f"Vx{tt}"])
            else:
                self.A("dve", lambda e, pi=pi, tt=tt: e.tensor_copy(out=Vx[:, tt, :, 0:64], in_=ps[pi][:, :].rearrange("p (h d) -> p h d", d=64)),
                       [f"ps{pi}"], [f"Vx{tt}"])
        slot, skey = slot_g
        for hp in range(4):
            for g in range(NG):
                pi = self.inproj_fm(slot, skey, hp * 128, g)
                self.A("act", lambda e, pi=pi, hp=hp, g=g: e.activation(out=sgB[:, hp, g * 512:(g + 1) * 512], in_=ps[pi][:, :], func=AF.Silu),
                       [f"ps{pi}"], [f"sgB{g}"])
        if self.dbg:
            qd = sbp("qd", [128, 4, 256], F32)
            kd = sbp("kd", [128, 4, 256], F32)
            self.A("dve", lambda e: e.tensor_copy(out=qd[:], in_=qT[:, :, 0:256]), [f"qT{hp}_0" for hp in range(4)], ["qd"])
            self.A("dve", lambda e: e.tensor_copy(out=kd[:], in_=kT[:, :, 0:256]), [f"kT{hp}_0" for hp in range(4)], ["kd"])
            self.dump("qT", qd[:], ["qd"], [128, 4, 256])
            self.dump("kT", kd[:], ["kd"], [128, 4, 256])
        self.mslots_pre = {0: self.load_w([(4232, 256), (5256, 256)], slot_i=0), 1: self.load_w([(4232 + 256, 256), (5256 + 256, 256)], slot_i=1)}
        R1s = [sbp(f"R1_{i}", [1, 512], BF16) for i in range(4)]
        ones1 = sbp("ones1", [1, 128], BF16)
        self.A("pool", lambda e: e.memset(ones1[:], 1.0), (), ["ones1"])
        NPT = 4
        Pt = [sbp(f"Pt{i}", [128, 512], BF16) for i in range(NPT)]
        otok = [sbp(f"otok{i}", [128, 4, 512], BF16) for i in range(2)]
        recs = sbp("recs", [128, 32], F32)
        items = [(G, h, kb) for G in range(4) for h in range(8) for kb in range(4 * G + 4)]
        LA = 3

        def s1(i):
            G, h, kb = items[i]
            hp = h // 2
            P = slice(64 * (h % 2), 64 * (h % 2) + 64)
            gh = G * 8 + h
            pa = 4 + (gh % 2)
            r1 = R1s[gh % 4]
            r1k = f"R1_{gh % 4}"
            if kb == 0:
                self.A("dve", lambda e: e.memset(ps[pa][:, 0:260], 0.0), (), [f"ps{pa}"])
                self.A("dve", lambda e: e.tensor_scalar(
                    out=r1[0:1, :].rearrange("p (q t) -> p q t", t=128),
                    in0=fl[2][0:1, 4 * G:4 * G + 4, h:h + 1].to_broadcast([1, 4, 128]),
                    scalar1=-1.0, scalar2=None, op0=ALU.mult), ["fl2"], [r1k])
            j0 = max(kb - 4 * G, 0)
            c0 = j0 * 128
            pi = i % 4
            pt = Pt[i % NPT]
            ptk = f"Pt{i % NPT}"
            self.mm(ps[pi][:, c0:512], kT[P, hp, kb * 128:(kb + 1) * 128], qT[P, hp, G * 512 + c0:(G + 1) * 512],
                    [f"kT{hp}_{kb // 4}", f"qT{hp}_{G}"], [f"ps{pi}"], start=True, stop=False)
            self.mm(ps[pi][:, c0:512], ones1[0:1, :], r1[0:1, c0:512], ["ones1", r1k], [f"ps{pi}"], start=False, stop=True)
            self.A("act", lambda e: e.activation(out=pt[:, c0:512], in_=ps[pi][:, c0:512], func=AF.Exp, bias=fl[4][:, kb, h:h + 1]),
                   [f"ps{pi}", "fl4"], [ptk])
            if kb >= 4 * G:
                self.A("pool", lambda e: e.tensor_tensor(out=pt[:, c0:c0 + 128], in0=pt[:, c0:c0 + 128], in1=self.M_fox[:], op=ALU.mult),
                       [ptk, "M_fox"], [ptk])

        def s2(i):
            G, h, kb = items[i]
            gh = G * 8 + h
            pa = 4 + (gh % 2)
            rc = (gh % 8) * 4
            ot = otok[G % 2]
            otk = f"otok{G % 2}"
            j0 = max(kb - 4 * G, 0)
            pt = Pt[i % NPT]
            ptk = f"Pt{i % NPT}"
            for j in range(j0, 4):
                self.mm(ps[pa][:, j * 65:(j + 1) * 65], pt[:, j * 128:(j + 1) * 128], Vx[:, kb, h, :], [ptk, f"Vx{kb}", "Vx1"], [f"ps{pa}"],
                        start=False, stop=(kb == 4 * G + j), skip=True)
            if kb == 4 * G + 3:
                self.A("dve", lambda e: e.reciprocal(out=recs[:, rc:rc + 4], in_=ps[pa][:, 0:260].rearrange("p (j d) -> p j d", d=65)[:, :, 64]),
                       [f"ps{pa}"], [f"recs{rc}"])
                self.A("dve", lambda e: e.tensor_tensor(
                    out=ot[:, :, h * 64:(h + 1) * 64], in0=ps[pa][:, 0:260].rearrange("p (j d) -> p j d", d=65)[:, :, 0:64],
                    in1=recs[:, rc:rc + 4].unsqueeze(2).to_broadcast([128, 4, 64]), op=ALU.mult), [f"ps{pa}", f"recs{rc}"], [otk])
                if h == 7:
                    for j in range(4):
                        qb = 4 * G + j
                        qs = slice(qb * 128, (qb + 1) * 128)
                        ptr = 6 + (qb % 2)
                        for hp in range(4):
                            self.tr(psb[ptr][:, hp * 128:(hp + 1) * 128], ot[:, j, hp * 128:(hp + 1) * 128], self.ident_b[:], [otk, "ident_b"], [f"ps{ptr}"])
                        self.A("dve", lambda e, ptr=ptr, qs=qs: e.tensor_tensor(
                            out=self.ybT[:, :, qs], in0=psb[ptr][:, 0:512].rearrange("p (a t) -> p a t", t=128), in1=sgB[:, :, qs], op=ALU.mult),
                            [f"ps{ptr}"] + [f"sgB{G}"], [f"ybT{G}"])

        n_it = len(items)
        for i in range(n_it + LA):
            if i < n_it:
                s1(i)
            if i - LA >= 0:
                s2(i - LA)
        if self.dbg:
            ybd = sbp("ybd", [128, 4, 512], F32)
            self.A("dve", lambda e: e.tensor_copy(out=ybd[:], in_=self.ybT[:, :, 512:1024]), ["ybT1"], ["ybd"])
            self.dump("ybT", ybd[:], ["ybd"], [128, 4, 512])

    def merge_out(self):
        nc, sbp, prm, ps, psb, hT = self.nc, self.sbp, self.prm, self.ps, self.psb, self.hT
        yaT, ybT = self.yaT, self.ybT
        mT = sbp("mT", [128, 8, T], BF16)
        woa, wob = self.woa, self.wob
        wo = sbp("wo", [128, 8, D], BF16)
        fm = [sbp(f"fm{i}", [128, 512], F32) for i in range(4)]
        mslots = dict(self.mslots_pre)
        self.A("pool", lambda e: e.dma_start(out=wo[:], in_=self.w_out.rearrange("(k p) c -> p k c", p=128)), (), ["wo"], dsem="wo")
        cnt = [0]

        def mgroup(slot, skey, dd, dc, g):
            gs = slice(g * 512, (g + 1) * 512)
            ia, ib = (cnt[0] % 2) * 2, (cnt[0] % 2) * 2 + 1
            cnt[0] += 1
            fa_, fb_ = fm[ia], fm[ib]
            ka, kb_ = f"fm{ia}", f"fm{ib}"
            p1 = self.inproj_fm(slot, skey, dd * 128, g)
            self.A("act", lambda e: e.activation(out=fa_[:], in_=ps[p1][:, :], func=AF.Sigmoid), [f"ps{p1}"], [ka])
            p2 = self.inproj_fm(slot, skey, 256 + dd * 128, g)
            self.A("act", lambda e: e.activation(out=fb_[:], in_=ps[p2][:, :], func=AF.Sigmoid), [f"ps{p2}"], [kb_])
            p3 = self.next_ps()
            for hp in range(4):
                self.mm(ps[p3][:, :], woa[:, hp, dc * 128:(dc + 1) * 128], yaT[:, hp, gs], ["woa", f"yaT{g}"], [f"ps{p3}"],
                        start=(hp == 0), stop=(hp == 3))
            p4 = self.next_ps()
            for hp in range(4):
                self.mm(ps[p4][:, :], wob[:, hp, dc * 128:(dc + 1) * 128], ybT[:, hp, gs], ["wob", f"ybT{g}"], [f"ps{p4}"],
                        start=(hp == 0), stop=(hp == 3))
            self.A("dve", lambda e: e.tensor_tensor(out=fa_[:], in0=ps[p3][:, :], in1=fa_[:], op=ALU.mult), [f"ps{p3}", ka], [ka])
            self.A("dve", lambda e: e.tensor_tensor(out=fb_[:], in0=ps[p4][:, :], in1=fb_[:], op=ALU.mult), [f"ps{p4}", kb_], [kb_])
            self.A("pool", lambda e: e.tensor_tensor(out=mT[:, dc, gs], in0=fa_[:], in1=fb_[:], op=ALU.add), [ka, kb_], [f"mT{g}"])

        for dcp in range(4):
            slot, skey = mslots[dcp]
            for dd in range(2):
                for g in range(NG):
                    mgroup(slot, skey, dd, dcp * 2 + dd, g)
            if dcp + 2 < 4:
                mslots[dcp + 2] = self.load_w([(4232 + (dcp + 2) * 256, 256), (5256 + (dcp + 2) * 256, 256)], slot_i=dcp % 2)
        if self.dbg:
            md = sbp("md", [128, 8, 256], F32)
            self.A("dve", lambda e: e.tensor_copy(out=md[:], in_=mT[:, :, 512:768]), ["mT1"], ["md"])
            self.dump("mT", md[:], ["md"], [128, 8, 256])
        NXR = 4
        xr = [sbp(f"xr{i}", [128, D], F32) for i in range(NXR)]
        oo = [sbp(f"oo{i}", [128, D], F32) for i in range(2)]
        junk2 = sbp("junk2", [128, D], BF16)
        st2 = sbp("st2", [128, 64], F32)
        x, out = self.x, self.out

        def ftile(tt):
            xt, xk = xr[tt % NXR], f"xr{tt % NXR}"
            ot, ok = oo[tt % 2], f"oo{tt % 2}"
            self.A("sp", lambda e: e.dma_start(out=xt[:], in_=x[tt * 128:(tt + 1) * 128, :]), (), [xk], dsem=xk)
            for half in range(2):
                pi = self.next_ps()
                for kc in range(8):
                    self.mm(ps[pi][:, :], mT[:, kc, tt * 128:(tt + 1) * 128], wo[:, kc, half * 512:(half + 1) * 512],
                            [f"mT{tt // 4}", "wo"], [f"ps{pi}"], start=(kc == 0), stop=(kc == 7))
                self.A("dve", lambda e, pi=pi, half=half: e.tensor_tensor(out=xt[:, half * 512:(half + 1) * 512], in0=ps[pi][:, :],
                                                                          in1=xt[:, half * 512:(half + 1) * 512], op=ALU.add),
                       [f"ps{pi}", xk], [xk])
            self.A("act", lambda e: e.activation(out=junk2[:], in_=xt[:], func=AF.Square, accum_out=st2[:, tt:tt + 1]), [xk], ["junk2", f"s2a{tt}"])
            self.A("dve", lambda e: e.tensor_scalar(out=st2[:, 16 + tt:17 + tt], in0=st2[:, tt:tt + 1], scalar1=1.0 / D, scalar2=RMS_EPS,
                                                    op0=ALU.mult, op1=ALU.add), [f"s2a{tt}"], [f"s2b{tt}"])
            self.A("act", lambda e: e.activation(out=st2[:, 32 + tt:33 + tt], in_=st2[:, 16 + tt:17 + tt], func=AF.Sqrt), [f"s2b{tt}"], [f"s2c{tt}"])
            self.A("dve", lambda e: e.reciprocal(out=st2[:, 48 + tt:49 + tt], in_=st2[:, 32 + tt:33 + tt]), [f"s2c{tt}"], [f"s2d{tt}"])
            self.A("dve", lambda e: e.scalar_tensor_tensor(out=ot[:], in0=xt[:], scalar=st2[:, 48 + tt:49 + tt], in1=self.fing_bc[:],
                                                           op0=ALU.mult, op1=ALU.mult), [xk, f"s2d{tt}", "params"], [ok])
            self.A("pool", lambda e: e.dma_start(out=out[tt * 128:(tt + 1) * 128, :], in_=ot[:]), [ok], [f"outw{tt}"], dsem=f"oo{tt % 2}")
            self.final_reads.append(f"outw{tt}")

        for tt in range(16):
            ftile(tt)

    def finish(self, out):
        if self.stage < 99:
            z = self.sbp("zout", [128, 8], F32)
            self.A("pool", lambda e: e.memset(z[:], 0.0), (), ["zout"])
            self.A("sp", lambda e: e.dma_start(out=out[0:128, 0:8], in_=z[:]), ["zout"], ["o_final"], dsem="o_final")
            self.final_reads.append("o_final")
        self.A("sp", lambda e: None, self.final_reads, ())
        self.S.emit(self.nc, self.st)
        self.ph.close()
        self.st.close()
        return self.nc


INPUT_ORDER = ["x", "norm_g", "w_in", "shift_mu", "w_lora_up", "w0", "a_lora_up", "a0", "k_k", "k_a", "r_k",
               "lnx_w", "lnx_b", "f_bias", "q_norm_g", "k_norm_g", "w_out_a", "w_out_b", "w_out", "final_norm_g"]


def make_in_maps(inputs, ncores=8):
    f = lambda a: np.ascontiguousarray(np.asarray(a, dtype=np.float32))
    shared = {
        "norm_g": f(inputs["norm_g"]).reshape(1, D),
        "w_in": f(inputs["w_in"]).reshape(D, IN_COLS),
        "shift_mu": f(inputs["shift_mu"]).reshape(1, 2176),
        "w_lora_up": f(inputs["w_lora_up"]).reshape(64, 512),
        "w0": f(inputs["w0"]).reshape(1, 512),
        "a_lora_up": f(inputs["a_lora_up"]).reshape(64, 512),
        "a0": f(inputs["a0"]).reshape(1, 512),
        "k_k": f(inputs["k_k"]).reshape(1, 512),
        "k_a": f(inputs["k_a"]).reshape(1, 512),
        "r_k": f(inputs["r_k"]).reshape(1, 512),
        "lnx_w": f(inputs["lnx_w"]).reshape(1, 512),
        "lnx_b": f(inputs["lnx_b"]).reshape(1, 512),
        "f_bias": f(inputs["f_bias"]).reshape(1, 8),
        "q_norm_g": f(inputs["q_norm_g"]).reshape(1, 64),
        "k_norm_g": f(inputs["k_norm_g"]).reshape(1, 64),
        "w_out_a": f(inputs["w_out_a"]).reshape(512, D),
        "w_out_b": f(inputs["w_out_b"]).reshape(512, D),
        "w_out": f(inputs["w_out"]).reshape(D, D),
        "final_norm_g": f(inputs["final_norm_g"]).reshape(1, D),
    }
    xs = f(inputs["x"])
    maps = []
    for c in range(ncores):
        m = dict(shared)
        m["x"] = np.ascontiguousarray(xs[c])
        maps.append(m)
    return maps


def kernel(**inputs):
    b = Builder()
    nc = b.build()
    in_maps = make_in_maps(inputs)
    res = run_bass_kernel_spmd(nc, in_maps, core_ids=list(range(8)))
    return np.stack([np.asarray(r["out"], dtype=np.float32) for r in res.results], axis=0)
```

```python
import os
from contextlib import ExitStack
import numpy as np
import concourse.bass as bass
import concourse.mybir as mybir
from concourse.bass_utils import run_bass_kernel_spmd

F32 = mybir.dt.float32
BF16 = mybir.dt.bfloat16
AF = mybir.ActivationFunctionType
ALU = mybir.AluOpType
AX = mybir.AxisListType

ENGS = ("pe", "act", "dve", "pool", "sp")
SEM_CH = 12000

T = 2048
D = 1024
NG = 4
C0 = 0.6065306597126334
RMS_EPS = 1e-6
LNX_EPS = 64e-5
IN_COLS = 6280


class Op:
    __slots__ = ("eng", "fn", "deps", "sig", "dsem", "dval", "idx", "cnt")


class Sched:
    def __init__(self):
        self.ops = {e: [] for e in ENGS}
        self.lastw = {}
        self.readers = {}
        self.dreaders = {}
        self.dcount = {}
        self.pending_barrier = {e: [] for e in ENGS}
        self.all_dma = []

    def add(self, eng, fn, reads=(), writes=(), dsem=None):
        op = Op()
        op.eng, op.fn, op.dsem, op.sig = eng, fn, dsem, False
        op.idx = len(self.ops[eng])
        op.cnt = None
        op.dval = None
        is_dma = dsem is not None
        if is_dma:
            self.dcount[dsem] = self.dcount.get(dsem, 0) + 1
            op.dval = 16 * self.dcount[dsem]
        deps = {}
        for r in reads:
            w = self.lastw.get(r)
            if w is not None:
                deps[id(w)] = w
        strict = eng != "pe"
        for wr in writes:
            lw = self.lastw.get(wr)
            if lw is not None and (lw.eng != eng or lw.dsem is not None or is_dma or strict):
                deps[id(lw)] = lw
            for rd in self.readers.get(wr, {}).values():
                if rd.eng != eng or is_dma or strict:
                    deps[id(rd)] = rd
            for rd in self.dreaders.get(wr, ()):
                deps[id(rd)] = rd
        for d in self.pending_barrier[eng]:
            deps[id(d)] = d
        self.pending_barrier[eng] = []
        op.deps = list(deps.values())
        for d in op.deps:
            if d.dsem is None:
                d.sig = True
        for r in reads:
            if is_dma:
                self.dreaders.setdefault(r, []).append(op)
            else:
                self.readers.setdefault(r, {})[eng] = op
        for wr in writes:
            self.lastw[wr] = op
            self.readers[wr] = {}
            self.dreaders[wr] = []
        self.ops[eng].append(op)
        if is_dma:
            self.all_dma.append(op)
        return op

    def barrier(self):
        lasts = []
        for e in ENGS:
            comp = [o for o in self.ops[e] if o.dsem is None]
            if comp:
                lasts.append(comp[-1])
        lastd = {}
        for o in self.all_dma:
            lastd[o.dsem] = o
        lasts.extend(lastd.values())
        for d in lasts:
            if d.dsem is None:
                d.sig = True
        for e in ENGS:
            self.pending_barrier[e] = list(lasts)

    def emit(self, nc, stack):
        for e in ENGS:
            c = 0
            for o in self.ops[e]:
                if o.dsem is None and o.sig:
                    c += 1
                    o.cnt = c
        nsem = {e: (max([o.cnt or 0 for o in self.ops[e]] + [0]) + SEM_CH - 1) // SEM_CH for e in ENGS}
        esems = {e: [stack.enter_context(nc.semaphore(f"s_{e}_{i}")) for i in range(nsem[e])] for e in ENGS}
        dsems = {k: stack.enter_context(nc.semaphore(f"d_{k}")) for k in self.dcount}
        block = stack.enter_context(nc.Block())

        def run(e, eh):
            waited = {}
            for o in self.ops[e]:
                need = {}
                for d in o.deps:
                    if d.dsem is not None:
                        k = ("d", d.dsem)
                        need[k] = max(need.get(k, 0), d.dval)
                    else:
                        k = ("e", d.eng)
                        need[k] = max(need.get(k, 0), d.cnt)
                for k, v in need.items():
                    if waited.get(k, 0) >= v:
                        continue
                    waited[k] = v
                    if k[0] == "d":
                        eh.wait_ge(dsems[k[1]], v)
                    else:
                        blk = (v - 1) // SEM_CH
                        eh.wait_ge(esems[k[1]][blk], v - blk * SEM_CH)
                ins = o.fn(eh)
                if ins is None:
                    continue
                if o.dsem is not None:
                    ins.then_inc(dsems[o.dsem], 16)
                elif o.sig:
                    blk = (o.cnt - 1) // SEM_CH
                    ins.then_inc(esems[e][blk], 1)

        @block.tensor
        def _(eh):
            run("pe", eh)

        @block.scalar
        def _(eh):
            run("act", eh)

        @block.vector
        def _(eh):
            run("dve", eh)

        @block.gpsimd
        def _(eh):
            run("pool", eh)

        @block.sync
        def _(eh):
            run("sp", eh)


class Builder:
    def __init__(self, stage=99, dbg=False):
        self.stage = stage
        self.dbg = dbg
        self.nc = bass.Bass("TRN2", target_bir_lowering=False)
        self.S = Sched()
        self.st = ExitStack()
        self.psn = 0
        self.cur = None
        self.rec = {0: [], 1: []}
        self.ps_range = {0: (0, 4), 1: (4, 4)}
        self.psn_s = {0: 0, 1: 0}
        self.ph = ExitStack()
        self.dbg_outs = {}
        self.uid = 0

    def sb(self, name, shape, dt):
        return self.st.enter_context(self.nc.sbuf_tensor(name, shape, dt))

    def sbp(self, name, shape, dt):
        return self.ph.enter_context(self.nc.sbuf_tensor(name, shape, dt))

    def end_phase(self):
        self.ph.close()
        self.ph = ExitStack()
        self.S.barrier()

    def dram_in(self, name, shape):
        return self.nc.dram_tensor(name, list(shape), F32, kind="ExternalInput").ap()

    def next_ps(self):
        if self.cur is not None:
            lo, n = self.ps_range[self.cur]
            i = lo + self.psn_s[self.cur] % n
            self.psn_s[self.cur] += 1
            return i
        i = self.psn % 8
        self.psn += 1
        return i

    def dump(self, name, ap, reads, shape):
        if not self.dbg:
            return
        dt = ap.dtype
        o = self.nc.dram_tensor("dbg_" + name, list(shape), dt, kind="ExternalOutput").ap()
        self.dbg_outs[name] = (shape, dt)
        self.A("sp", lambda e: e.dma_start(out=o, in_=ap), reads, ["dbgout_" + name], dsem="dbg_" + name)
        self.final_reads.append("dbgout_" + name)

    def A(self, eng, fn, r=(), w=(), dsem=None):
        if self.cur is not None:
            self.rec[self.cur].append((eng, fn, tuple(r), tuple(w), dsem))
            return None
        return self.S.add(eng, fn, reads=r, writes=w, dsem=dsem)

    def mm(self, out, lhsT, rhs, r, w, start=True, stop=True, skip=False):
        if skip:
            self.A("pe", lambda e: e.matmul(out, lhsT=lhsT, rhs=rhs, start=start, stop=stop, skip_group_check=True), r, w)
        else:
            self.A("pe", lambda e: e.matmul(out, lhsT=lhsT, rhs=rhs, start=start, stop=stop), r, w)

    def tr(self, out, in_, ident, r, w):
        self.A("pe", lambda e: e.transpose(out, in_, ident), r, w)

    def build(self):
        nc, S = self.nc, self.S
        self.final_reads = []
        x = self.dram_in("x", [T, D])
        norm_g = self.dram_in("norm_g", [1, D])
        w_in = self.dram_in("w_in", [D, IN_COLS])
        shift_mu = self.dram_in("shift_mu", [1, 2176])
        w_lora_up = self.dram_in("w_lora_up", [64, 512])
        w0 = self.dram_in("w0", [1, 512])
        a_lora_up = self.dram_in("a_lora_up", [64, 512])
        a0 = self.dram_in("a0", [1, 512])
        k_k = self.dram_in("k_k", [1, 512])
        k_a = self.dram_in("k_a", [1, 512])
        r_k = self.dram_in("r_k", [1, 512])
        lnx_w = self.dram_in("lnx_w", [1, 512])
        lnx_b = self.dram_in("lnx_b", [1, 512])
        f_bias = self.dram_in("f_bias", [1, 8])
        q_norm_g = self.dram_in("q_norm_g", [1, 64])
        k_norm_g = self.dram_in("k_norm_g", [1, 64])
        w_out_a = self.dram_in("w_out_a", [512, D])
        w_out_b = self.dram_in("w_out_b", [512, D])
        w_out = self.dram_in("w_out", [D, D])
        final_g = self.dram_in("final_norm_g", [1, D])
        out = nc.dram_tensor("out", [T, D], F32, kind="ExternalOutput").ap()
        self.w_in = w_in

        sb = self.sb
        self.ps = [self.st.enter_context(nc.psum_tensor(f"ps{i}", [128, 512], F32)) for i in range(8)]
        self.psb = [p.bitcast(BF16) for p in self.ps]
        hT = sb("hT", [128, 8, T], BF16)
        self.hT = hT
        ident_f = sb("ident_f", [128, 128], F32)
        ident_b = sb("ident_b", [128, 128], BF16)
        bones = sb("bones", [128, 128], F32)
        M_lt = sb("M_lt", [128, 8, 64], F32)
        M_le = sb("M_le", [128, 8, 64], F32)
        M_gt = sb("M_gt", [128, 8, 64], F32)
        M_fox = sb("M_fox", [128, 128], BF16)
        rst = sb("rst", [128, 512], BF16)
        prm = sb("prm", [128, 96], F32)
        fb_bc = sb("fb_bc", [128, 8], F32)
        fing_bc = sb("fing_bc", [128, D], F32)
        wup = sb("wup", [128, 512], BF16)
        bones_b = sb("bones_b", [128, 128], BF16)
        self.bones_b = bones_b
        self.ident_f, self.ident_b, self.bones = ident_f, ident_b, bones
        self.prm = prm
        MU, OM, W0c, A0c, KKc, KAc, OMKA, LNW, LNB, RKc, QG, KG, NGc = 0, 17, 34, 38, 42, 46, 50, 54, 58, 62, 66, 67, 68

        nparam = [0]

        def pload(dst, src):
            def f(e):
                with nc.allow_non_contiguous_dma(reason="small param load"):
                    return e.dma_start(out=dst, in_=src)
            self.A("sp", f, (), ["params"], dsem="params")

        pload(fb_bc[:], f_bias.partition_broadcast(128))
        pload(fing_bc[:], final_g.partition_broadcast(128))
        prow = sb("prow", [76, 128], F32)
        self.A("pool", lambda e: e.memset(prow[:], 0.0), (), ["prow"])

        def rload(dst, src):
            self.A("sp", lambda e: e.dma_start(out=dst, in_=src), (), ["prow"], dsem="prow")

        rload(prow[MU:MU + 17, :], shift_mu.rearrange("o (c p) -> (o c) p", p=128))
        for col, src in ((W0c, w0), (A0c, a0), (KKc, k_k), (KAc, k_a), (LNW, lnx_w), (LNB, lnx_b), (RKc, r_k)):
            rload(prow[col:col + 4, :], src.rearrange("o (c p) -> (o c) p", p=128))
        for hh in range(2):
            rload(prow[QG:QG + 1, 64 * hh:64 * hh + 64], q_norm_g)
            rload(prow[KG:KG + 1, 64 * hh:64 * hh + 64], k_norm_g)
        rload(prow[NGc:NGc + 8, :], norm_g.rearrange("o (c p) -> (o c) p", p=128))
        self._param_transpose = (prow, prm)
        self.A("pool", lambda e: e.dma_start(out=wup[0:64, :], in_=w_lora_up), (), ["wup"], dsem="wup")
        self.A("pool", lambda e: e.dma_start(out=wup[64:128, :], in_=a_lora_up), (), ["wup"], dsem="wup")
        PR = ["params"]

        self.A("pool", lambda e: e.memset(ident_f[:], 1.0), (), ["ident_f"])
        self.A("pool", lambda e: e.affine_select(out=ident_f[:], in_=ident_f[:], pattern=[[-1, 128]],
                                                  compare_op=ALU.is_equal, fill=0.0, base=0, channel_multiplier=1),
               ["ident_f"], ["ident_f"])
        self.A("dve", lambda e: e.tensor_copy(out=ident_b[:], in_=ident_f[:]), ["ident_f"], ["ident_b"])
        pti = self.next_ps()
        self.tr(self.ps[pti][:, 0:76], prow[0:76, :], ident_f[0:76, 0:76], ["prow", "ident_f"], [f"ps{pti}"])
        self.A("dve", lambda e: e.tensor_copy(out=prm[:, 0:76], in_=self.ps[pti][:, 0:76]), [f"ps{pti}"], ["params"])
        self.A("pool", lambda e: e.memset(bones[:], 0.0), (), ["bones"])
        for hh in range(2):
            self.A("pool", lambda e, hh=hh: e.memset(bones[64 * hh:64 * hh + 64, 64 * hh:64 * hh + 64], 1.0), ["bones"], ["bones"])
        self.A("dve", lambda e: e.tensor_copy(out=bones_b[:], in_=bones[:]), ["bones"], ["bones_b"])
        for (mt, cm, st_, op, nm) in ((M_lt, -1, 1, ALU.is_gt, "M_lt"), (M_le, -1, 1, ALU.is_ge, "M_le"), (M_gt, 1, -1, ALU.is_gt, "M_gt")):
            self.A("pool", lambda e, mt=mt: e.memset(mt[:], 1.0), (), [nm])
            for hh in range(2):
                self.A("pool", lambda e, mt=mt, cm=cm, st_=st_, op=op, hh=hh: e.affine_select(
                    out=mt[64 * hh:64 * hh + 64], in_=mt[64 * hh:64 * hh + 64], pattern=[[0, 8], [st_, 64]],
                    compare_op=op, fill=0.0, base=0, channel_multiplier=cm), [nm], [nm])
        mfox_f = sb("mfox_f", [128, 128], F32)
        self.mfox_f = mfox_f
        self.A("pool", lambda e: e.memset(mfox_f[:], 1.0), (), ["mfox_f"])
        self.A("pool", lambda e: e.affine_select(out=mfox_f[:], in_=mfox_f[:], pattern=[[1, 128]], compare_op=ALU.is_ge,
                                                  fill=0.0, base=0, channel_multiplier=-1), ["mfox_f"], ["mfox_f"])
        self.A("dve", lambda e: e.tensor_copy(out=M_fox[:], in_=mfox_f[:]), ["mfox_f"], ["M_fox"])
        self.A("pool", lambda e: e.memset(rst[:], 1.0), (), ["rst"])
        self.A("pool", lambda e: e.memset(rst[:].rearrange("p (c t) -> p c t", t=64)[:, :, 0:1], 0.0), ["rst"], ["rst"])
        self.A("dve", lambda e: e.tensor_scalar(out=prm[:, OM:OM + 17], in0=prm[:, MU:MU + 17], scalar1=-1.0, scalar2=1.0,
                                                 op0=ALU.mult, op1=ALU.add), PR, ["prm2"])
        self.A("dve", lambda e: e.tensor_scalar(out=prm[:, OMKA:OMKA + 4], in0=prm[:, KAc:KAc + 4], scalar1=-1.0, scalar2=1.0,
                                                 op0=ALU.mult, op1=ALU.add), PR, ["prm2"])
        PR2 = ["params", "prm2"]

        self.wsl = [sb(f"wsl{i}", [128, 8, 512], BF16) for i in range(2)]
        self.wsn = 0
        self.pre_lora = self.load_w([(1536, 128)], slot_i=1)
        self.pre_s0 = self.load_w([(0, 128), (512, 128), (1024, 128), (1664, 128)], slot_i=0)
        sbp = self.sbp
        xsl = [sbp(f"xsl{i}", [128, D], F32) for i in range(2)]
        xnb = [sbp(f"xnb{i}", [128, D], BF16) for i in range(2)]
        junk = sbp("junk", [128, D], BF16)
        stat = sbp("stat", [128, 64], F32)
        for tt in range(16):
            xs = xsl[tt % 2]
            xk = f"xsl{tt % 2}"
            xn = xnb[tt % 2]
            nk = f"xnb{tt % 2}"
            self.A("sp", lambda e, xs=xs, tt=tt: e.dma_start(out=xs[:], in_=x[tt * 128:(tt + 1) * 128, :]), (), [xk], dsem=xk)
            self.A("act", lambda e, xs=xs, tt=tt: e.activation(out=junk[:], in_=xs[:], func=AF.Square,
                                                                accum_out=stat[:, tt:tt + 1]), [xk], ["junk", f"st{tt}"])
            self.A("dve", lambda e, tt=tt: e.tensor_scalar(out=stat[:, 16 + tt:17 + tt], in0=stat[:, tt:tt + 1], scalar1=1.0 / D,
                                                           scalar2=RMS_EPS, op0=ALU.mult, op1=ALU.add), [f"st{tt}"], [f"st{tt}b"])
            self.A("act", lambda e, tt=tt: e.activation(out=stat[:, 32 + tt:33 + tt], in_=stat[:, 16 + tt:17 + tt], func=AF.Sqrt),
                   [f"st{tt}b"], [f"st{tt}c"])
            self.A("dve", lambda e, tt=tt: e.reciprocal(out=stat[:, 48 + tt:49 + tt], in_=stat[:, 32 + tt:33 + tt]),
                   [f"st{tt}c"], [f"st{tt}d"])
            self.A("dve", lambda e, xs=xs, xn=xn, tt=tt: e.tensor_scalar(out=xn[:], in0=xs[:], scalar1=stat[:, 48 + tt:49 + tt],
                                                                         scalar2=None, op0=ALU.mult), [xk, f"st{tt}d"], [nk])
            pi = self.next_ps()
            for kc in range(8):
                self.tr(self.psb[pi][:, kc * 128:(kc + 1) * 128], xn[:, kc * 128:(kc + 1) * 128], ident_b[:],
                        [nk, "ident_b"], [f"ps{pi}"])
            self.A("dve", lambda e, pi=pi, tt=tt: e.tensor_tensor(
                out=hT[:, :, tt * 128:(tt + 1) * 128],
                in0=self.psb[pi][:, 0:1024].rearrange("p (k t) -> p k t", t=128),
                in1=prm[:, NGc:NGc + 8].unsqueeze(2).to_broadcast([128, 8, 128]), op=ALU.mult),
                [f"ps{pi}"] + PR, [f"hT{tt // 4}"])
        if self.dbg:
            hdump = sbp("hdump", [128, 8, 256], F32)
            self.A("dve", lambda e: e.tensor_copy(out=hdump[:], in_=hT[:, :, 0:256]), ["hT0"], ["hdump"])
            self.dump("hT", hdump[:], ["hdump"], [128, 8, 256])
        self.x, self.out, self.fing_bc, self.fb_bc, self.M_fox = x, out, fing_bc, fb_bc, M_fox
        self.w_out_a, self.w_out_b, self.w_out = w_out_a, w_out_b, w_out
        self.QG, self.KG = QG, KG
        if self.stage <= 1:
            return self.finish(out)
        self.end_phase()

        self.yaT = sb("yaT", [128, 4, T], BF16)
        self.ones_f = sb("ones_f", [128, 128], F32)
        self.fl = [sb(f"fl{i}", [128, 16, 8], F32) for i in range(5)]
        self.wf = sb("wf", [128, 8, 8], BF16)
        self.PR2 = PR2
        if self.stage == 6:
            self.c0()
        if self.stage != 6:
            self.rwkv(PR2, MU, OM, W0c, A0c, KKc, KAc, OMKA, LNW, LNB, RKc, wup, M_lt, M_le, M_gt, rst)
        if self.stage <= 5:
            return self.finish(out)
        self.end_phase()
        self.ybT = sb("ybT", [128, 4, T], BF16)
        self.woa = sb("woa", [128, 4, D], BF16)
        self.wob = sb("wob", [128, 4, D], BF16)
        self.A("pool", lambda e: e.dma_start(out=self.woa[:], in_=self.w_out_a.rearrange("(k p) c -> p k c", p=128)), (), ["woa"], dsem="woa")
        self.A("pool", lambda e: e.dma_start(out=self.wob[:], in_=self.w_out_b.rearrange("(k p) c -> p k c", p=128)), (), ["wob"], dsem="wob")
        self.fox()
        if self.stage <= 6:
            return self.finish(out)
        self.end_phase()
        self.merge_out()
        return self.finish(out)

    def load_w(self, pieces, slot_i=None):
        if slot_i is None:
            i = self.wsn % 2
            self.wsn += 1
        else:
            i = slot_i
        slot = self.wsl[i]
        off = 0
        for (c0, n) in pieces:
            def f(e, c0=c0, n=n, off=off):
                return e.dma_start(out=slot[:, :, off:off + n],
                                   in_=self.w_in[:, c0:c0 + n].rearrange("(k p) c -> p k c", p=128))
            self.A("pool", f, (), [f"wsl{i}"], dsem=f"wsl{i}")
            off += n
        return slot, f"wsl{i}"

    def inproj_fm(self, slot, skey, off, g, M=128):
        pi = self.next_ps()
        for kc in range(8):
            self.mm(self.ps[pi][0:M, :], slot[:, kc, off:off + M], self.hT[:, kc, g * 512:(g + 1) * 512],
                    [skey, f"hT{g}"], [f"ps{pi}"], start=(kc == 0), stop=(kc == 7))
        return pi

    def rwkv(self, PR2, MU, OM, W0c, A0c, KKc, KAc, OMKA, LNW, LNB, RKc, wup, M_lt, M_le, M_gt, rst):
        nc, sb, prm = self.nc, self.sbp, self.prm
        ps, psb = self.ps, self.psb
        NTS = 4
        tsb = [sb(f"tsb{i}", [128, 514], F32) for i in range(NTS)]
        self.tsn = 0
        self.tsn_s = {0: 0, 1: 0}
        carry = sb("carry", [128, 16], F32)
        self.A("pool", lambda e: e.memset(carry[:], 0.0), (), ["carry"])

        def shift_evac(pi, mu_col, stream, g, dst, dkey, npart=128):
            if self.cur is not None:
                i = self.cur * 2 + self.tsn_s[self.cur] % 2
                self.tsn_s[self.cur] += 1
            else:
                i = self.tsn % NTS
                self.tsn += 1
            ts = tsb[i]
            tk = f"tsb{i}"
            P = slice(0, npart)
            self.A("act", lambda e: e.activation(out=ts[P, 1:513], in_=ps[pi][P, :], func=AF.Copy,
                                                  scale=prm[P, MU + mu_col:MU + mu_col + 1]), [f"ps{pi}"] + PR2, [tk])
            if g == 0:
                self.A("pool", lambda e: e.memset(ts[P, 0:1], 0.0), [tk], [tk])
            else:
                self.A("pool", lambda e: e.tensor_copy(out=ts[P, 0:1], in_=carry[P, stream:stream + 1]), [tk, f"carry{stream}"], [tk])
            self.A("dve", lambda e: e.scalar_tensor_tensor(out=dst, in0=ps[pi][P, :], scalar=prm[P, OM + mu_col:OM + mu_col + 1],
                                                           in1=ts[P, 0:512], op0=ALU.mult, op1=ALU.add),
                   [f"ps{pi}", tk] + PR2, [dkey])
            if g < 3:
                self.A("pool", lambda e: e.tensor_copy(out=carry[P, stream:stream + 1], in_=ts[P, 512:513]), [tk], [f"carry{stream}"])

        twad = sb("twad", [128, T], BF16)
        ltmp = sb("ltmp", [128, 512], F32)
        slot, skey = self.pre_lora
        for g in range(NG):
            pi = self.inproj_fm(slot, skey, 0, g)
            shift_evac(pi, 12, 0, g, ltmp[:], "ltmp")
            self.A("act", lambda e, g=g: e.activation(out=twad[0:64, g * 512:(g + 1) * 512], in_=ltmp[0:64, :], func=AF.Tanh),
                   ["ltmp"], [f"twad{g}"])
            self.A("act", lambda e, g=g: e.activation(out=twad[64:128, g * 512:(g + 1) * 512], in_=ltmp[64:128, :], func=AF.Copy),
                   ["ltmp"], [f"twad{g}"])

        self.c0()
        NF, NH = 28, 46
        Fp = [sb(f"F{i}", [128, 512], F32) for i in range(NF)]
        Hp = [sb(f"H{i}", [128, 512], BF16) for i in range(NH)]
        ffree_s = {0: list(range(0, NF // 2)), 1: list(range(NF // 2, NF))}
        hfree_s = {0: list(range(0, NH // 2)), 1: list(range(NH // 2, NH))}

        class Buf:
            pass

        def fa():
            ffree = ffree_s[self.cur]
            i = ffree.pop(0)
            b = Buf()
            b.t, b.k, b.i, b.pool = Fp[i], f"F{i}", i, ffree
            return b

        def ha():
            hfree = hfree_s[self.cur]
            i = hfree.pop(0)
            b = Buf()
            b.t, b.k, b.i, b.pool = Hp[i], f"H{i}", i, hfree
            return b

        def rel(*bs):
            for b in bs:
                b.pool.append(b.i)

        yaT = self.yaT
        Hf_s = [sb(f"Hf{i}", [128, 64], F32) for i in range(2)]
        Hb_s = [sb(f"Hb{i}", [128, 64], BF16) for i in range(2)]
        Xh_s = [sb(f"Xh{i}", [128, 64], F32) for i in range(2)]
        pcs_s = [sb(f"pcs{i}", [128, 32], F32) for i in range(2)]
        gst_s = [sb(f"gst{i}", [128, 64], F32) for i in range(2)]

        def v3(ap, inner):
            return ap.rearrange("p (c t) -> p c t", t=inner)

        def bmm(lh, rh, outcols=64):
            pi = self.next_ps()
            for c in range(8):
                for hh in range(2):
                    P = slice(64 * hh, 64 * hh + 64)
                    self.mm(ps[pi][P, c * 64:(c + 1) * 64], lh.t[P, c * 64:(c + 1) * 64], rh.t[P, c * 64:(c + 1) * 64],
                            [lh.k, rh.k], [f"ps{pi}"])
            return pi

        def btr(src, col0, stride):
            pi = self.next_ps()
            for c in range(8):
                for hh in range(2):
                    P = slice(64 * hh, 64 * hh + 64)
                    self.tr(psb[pi][P, c * 64:(c + 1) * 64], src.t[P, c * stride + col0:c * stride + col0 + 64],
                            self.ident_b[P, P], [src.k, "ident_b"], [f"ps{pi}"])
            return pi


        def batch(hp, g, slot, skey, sidx):
            pcs = pcs_s[sidx][:, g * 8:(g + 1) * 8]
            kpcs = f"pcs{sidx}_{g}"
            if True:
                gs = slice(g * 512, (g + 1) * 512)
                Fr, Fk, Fv, Fg = fa(), fa(), fa(), fa()
                for j, (dst, mc) in enumerate(((Fr, hp), (Fk, 4 + hp), (Fv, 8 + hp), (Fg, 13 + hp))):
                    pi = self.inproj_fm(slot, skey, j * 128, g)
                    shift_evac(pi, mc, 1 + sidx * 4 + j, g, dst.t[:], dst.k)
                    yield
                Fs, Fa_ = fa(), fa()
                pi = self.next_ps()
                self.mm(ps[pi][:, :], wup[0:64, hp * 128:(hp + 1) * 128], twad[0:64, gs], ["wup", f"twad{g}"], [f"ps{pi}"])
                self.A("act", lambda e, pi=pi, Fs=Fs: e.activation(out=Fs.t[:], in_=ps[pi][:, :], func=AF.Sigmoid,
                                                                     bias=prm[:, W0c + hp:W0c + hp + 1]), [f"ps{pi}"] + PR2, [Fs.k])
                pi = self.next_ps()
                self.mm(ps[pi][:, :], wup[64:128, hp * 128:(hp + 1) * 128], twad[64:128, gs], ["wup", f"twad{g}"], [f"ps{pi}"])
                self.A("act", lambda e, pi=pi, Fa_=Fa_: e.activation(out=Fa_.t[:], in_=ps[pi][:, :], func=AF.Sigmoid,
                                                                       bias=prm[:, A0c + hp:A0c + hp + 1]), [f"ps{pi}"] + PR2, [Fa_.k])
                if self.dbg and hp == 0 and g == 1:
                    self.dump("r_mixed", Fr.t[:], [Fr.k], [128, 512])
                    self.dump("k_mixed", Fk.t[:], [Fk.k], [128, 512])
                    self.dump("sig", Fs.t[:], [Fs.k], [128, 512])
                    self.dump("a", Fa_.t[:], [Fa_.k], [128, 512])
                yield
                Fkk, Ft1 = fa(), fa()
                self.A("dve", lambda e, Fkk=Fkk, Fk=Fk: e.tensor_scalar(out=Fkk.t[:], in0=Fk.t[:], scalar1=prm[:, KKc + hp:KKc + hp + 1],
                                                                         scalar2=None, op0=ALU.mult), [Fk.k] + PR2, [Fkk.k])
                Hsq = ha()
                self.A("act", lambda e, Hsq=Hsq, Fkk=Fkk: e.activation(out=Hsq.t[:], in_=Fkk.t[:], func=AF.Square), [Fkk.k], [Hsq.k])
                pi = self.next_ps()
                self.mm(ps[pi][:, :], self.bones_b[:], Hsq.t[:], ["bones_b", Hsq.k], [f"ps{pi}"])
                rel(Hsq)
                self.A("act", lambda e, pi=pi, Ft1=Ft1: e.activation(out=Ft1.t[:], in_=ps[pi][:, :], func=AF.Sqrt), [f"ps{pi}"], [Ft1.k])
                self.A("dve", lambda e, Ft1=Ft1: e.tensor_scalar_max(out=Ft1.t[:], in0=Ft1.t[:], scalar1=1e-12), [Ft1.k], [Ft1.k])
                self.A("dve", lambda e, Ft1=Ft1: e.reciprocal(out=Ft1.t[:], in_=Ft1.t[:]), [Ft1.k], [Ft1.k])
                self.A("dve", lambda e, Fkk=Fkk, Ft1=Ft1: e.tensor_tensor(out=Fkk.t[:], in0=Fkk.t[:], in1=Ft1.t[:], op=ALU.mult),
                       [Fkk.k, Ft1.k], [Fkk.k])
                if self.dbg and hp == 0 and g == 1:
                    self.dump("kkn", Fkk.t[:], [Fkk.k], [128, 512])
                self.A("dve", lambda e, Ft1=Ft1, Fa_=Fa_: e.tensor_scalar(out=Ft1.t[:], in0=Fa_.t[:], scalar1=prm[:, KAc + hp:KAc + hp + 1],
                                                                            scalar2=prm[:, OMKA + hp:OMKA + hp + 1], op0=ALU.mult, op1=ALU.add),
                       [Fa_.k] + PR2, [Ft1.k])
                self.A("dve", lambda e, Fk=Fk, Ft1=Ft1: e.tensor_tensor(out=Fk.t[:], in0=Fk.t[:], in1=Ft1.t[:], op=ALU.mult),
                       [Fk.k, Ft1.k], [Fk.k])
                self.A("dve", lambda e, Fa_=Fa_, Fkk=Fkk: e.tensor_tensor(out=Fa_.t[:], in0=Fa_.t[:], in1=Fkk.t[:], op=ALU.mult),
                       [Fa_.k, Fkk.k], [Fa_.k])
                yield
                Hv = ha()
                self.A("act", lambda e, Hv=Hv, Fv=Fv: e.activation(out=Hv.t[:], in_=Fv.t[:], func=AF.Copy), [Fv.k], [Hv.k])
                Hrk = ha()
                self.A("dve", lambda e, Hrk=Hrk, Fr=Fr, Fk=Fk: e.scalar_tensor_tensor(out=Hrk.t[:], in0=Fr.t[:], scalar=prm[:, RKc + hp:RKc + hp + 1],
                                                                                     in1=Fk.t[:], op0=ALU.mult, op1=ALU.mult),
                       [Fr.k, Fk.k] + PR2, [Hrk.k])
                pi = self.next_ps()
                self.mm(ps[pi][:, :], self.bones_b[:], Hrk.t[:], ["bones_b", Hrk.k], [f"ps{pi}"])
                rel(Hrk)
                self.A("dve", lambda e, pi=pi, Fv=Fv: e.tensor_tensor(out=Fv.t[:], in0=ps[pi][:, :], in1=Fv.t[:], op=ALU.mult),
                       [f"ps{pi}", Fv.k, Hv.k], [Fv.k])
                self.A("act", lambda e, Fg=Fg: e.activation(out=Fg.t[:], in_=Fg.t[:], func=AF.Silu), [Fg.k], [Fg.k])
                self.A("dve", lambda e, Ft1=Ft1, Fs=Fs: e.tensor_tensor_scan(out=Ft1.t[:], data0=rst[:], data1=Fs.t[:], initial=0.0,
                                                                              op0=ALU.mult, op1=ALU.add), ["rst", Fs.k], [Ft1.k])
                if self.dbg and hp == 0 and g == 1:
                    self.dump("Ls", Ft1.t[:], [Ft1.k], [128, 512])
                    self.dump("k2", Fk.t[:], [Fk.k], [128, 512])
                Fe = fa()
                Hrt, Hat, Hbt, Hkt, Hbh, Hkh = ha(), ha(), ha(), ha(), ha(), ha()
                self.A("act", lambda e, Fe=Fe, Ft1=Ft1: e.activation(out=Fe.t[:], in_=Ft1.t[:], func=AF.Exp, scale=-C0), [Ft1.k], [Fe.k])
                self.A("dve", lambda e, Hrt=Hrt, Fr=Fr, Fe=Fe: e.tensor_tensor(out=Hrt.t[:], in0=Fr.t[:], in1=Fe.t[:], op=ALU.mult),
                       [Fr.k, Fe.k], [Hrt.k])
                self.A("dve", lambda e, Fe=Fe: e.tensor_copy(out=pcs[:, :], in_=v3(Fe.t[:], 64)[:, :, 63]), [Fe.k], [kpcs])
                yield
                self.A("act", lambda e, Fr=Fr, Ft1=Ft1: e.activation(out=Fr.t[:], in_=Ft1.t[:], func=AF.Exp, scale=C0), [Ft1.k, Hrt.k], [Fr.k])
                self.A("dve", lambda e, Hbt=Hbt, Fa_=Fa_, Fr=Fr: e.tensor_tensor(out=Hbt.t[:], in0=Fa_.t[:], in1=Fr.t[:], op=ALU.mult),
                       [Fa_.k, Fr.k], [Hbt.k])
                self.A("dve", lambda e, Hkt=Hkt, Fk=Fk, Fr=Fr: e.tensor_tensor(out=Hkt.t[:], in0=Fk.t[:], in1=Fr.t[:], op=ALU.mult),
                       [Fk.k, Fr.k], [Hkt.k])
                self.A("dve", lambda e, Fs=Fs, Ft1=Ft1: e.tensor_tensor(out=Fs.t[:], in0=Ft1.t[:], in1=Fs.t[:], op=ALU.subtract),
                       [Ft1.k, Fs.k], [Fs.k])
                self.A("act", lambda e, Fe=Fe, Fs=Fs: e.activation(out=Fe.t[:], in_=Fs.t[:], func=AF.Exp, scale=-C0), [Fs.k, Hrt.k, kpcs], [Fe.k])
                self.A("dve", lambda e, Hat=Hat, Fkk=Fkk, Fe=Fe: e.scalar_tensor_tensor(out=Hat.t[:], in0=Fkk.t[:], scalar=-1.0, in1=Fe.t[:],
                                                                                        op0=ALU.mult, op1=ALU.mult), [Fkk.k, Fe.k], [Hat.k])
                yield
                self.A("dve", lambda e, Fs=Fs, Ft1=Ft1: e.tensor_tensor(out=v3(Fs.t[:], 64), in0=v3(Ft1.t[:], 64)[:, :, 63:64].to_broadcast([128, 8, 64]),
                                                                        in1=v3(Ft1.t[:], 64), op=ALU.subtract), [Ft1.k, Fs.k], [Fs.k])
                self.A("act", lambda e, Fe=Fe, Fs=Fs: e.activation(out=Fe.t[:], in_=Fs.t[:], func=AF.Exp, scale=-C0), [Fs.k, Hat.k], [Fe.k])
                self.A("dve", lambda e, Hbh=Hbh, Fa_=Fa_, Fe=Fe: e.tensor_tensor(out=Hbh.t[:], in0=Fa_.t[:], in1=Fe.t[:], op=ALU.mult),
                       [Fa_.k, Fe.k], [Hbh.k])
                self.A("dve", lambda e, Hkh=Hkh, Fk=Fk, Fe=Fe: e.tensor_tensor(out=Hkh.t[:], in0=Fk.t[:], in1=Fe.t[:], op=ALU.mult),
                       [Fk.k, Fe.k], [Hkh.k])
                if self.dbg and hp == 0 and g == 1:
                    for nm, b in (("rt", Hrt), ("at", Hat), ("bt", Hbt), ("kt", Hkt), ("bh", Hbh), ("kh", Hkh), ("vb", Hv)):
                        self.dump(nm, b.t[:], [b.k], [128, 512])
                    self.dump("bonus", Fv.t[:], [Fv.k], [128, 512])
                    self.dump("sg", Fg.t[:], [Fg.k], [128, 512])
                rel(Fr, Fk, Fs, Fa_, Fkk, Ft1, Fe)
                if self.stage <= 2:
                    rel(Fv, Fg, Hrt, Hat, Hbt, Hkt, Hbh, Hkh, Hv)
                    return

                yield
                N0T, N0, LrbT, LrkT = ha(), ha(), ha(), ha()
                pi = bmm(Hbt, Hat)
                self.A("dve", lambda e, pi=pi, N0T=N0T: e.tensor_tensor(out=N0T.t[:], in0=ps[pi][:, :], in1=M_lt[:].rearrange("p c t -> p (c t)"), op=ALU.mult),
                       [f"ps{pi}", "M_lt"], [N0T.k])
                pi = bmm(Hat, Hbt)
                self.A("dve", lambda e, pi=pi, N0=N0: e.tensor_tensor(out=N0.t[:], in0=ps[pi][:, :], in1=M_gt[:].rearrange("p c t -> p (c t)"), op=ALU.mult),
                       [f"ps{pi}", "M_gt"], [N0.k])
                pi = bmm(Hbt, Hrt)
                self.A("dve", lambda e, pi=pi, LrbT=LrbT: e.tensor_tensor(out=LrbT.t[:], in0=ps[pi][:, :], in1=M_le[:].rearrange("p c t -> p (c t)"), op=ALU.mult),
                       [f"ps{pi}", "M_le"], [LrbT.k])
                pi = bmm(Hkt, Hrt)
                self.A("dve", lambda e, pi=pi, LrkT=LrkT: e.tensor_tensor(out=LrkT.t[:], in0=ps[pi][:, :], in1=M_le[:].rearrange("p c t -> p (c t)"), op=ALU.mult),
                       [f"ps{pi}", "M_le"], [LrkT.k])
                Yb = [ha(), ha()]

                def yv(b):
                    return b.t[:].rearrange("p (c n) -> p c n", n=128)

                pi = btr(Hat, 0, 64)
                for half in range(2):
                    self.A("act", lambda e, pi=pi, half=half, d=Yb[half]: e.activation(out=yv(d)[:, :, 0:64],
                                                                            in_=v3(psb[pi][:, 0:512], 64)[:, half * 4:half * 4 + 4, :], func=AF.Copy),
                           [f"ps{pi}"], [Yb[half].k])
                pi = bmm(Hat, Hkt)
                for half in range(2):
                    self.A("dve", lambda e, pi=pi, half=half, d=Yb[half]: e.tensor_tensor(out=yv(d)[:, :, 64:128],
                                                                              in0=v3(ps[pi][:, :], 64)[:, half * 4:half * 4 + 4, :],
                                                                              in1=M_gt[:, half * 4:half * 4 + 4, :], op=ALU.mult),
                           [f"ps{pi}", "M_gt"], [Yb[half].k])
                yield
                Btk, Ktk, Vtk = ha(), ha(), ha()
                for ii, (src, dst) in enumerate(((Hbh, Btk), (Hkh, Ktk), (Hv, Vtk))):
                    pi = btr(src, 0, 64)
                    if ii == 1:
                        self.A("dve", lambda e, pi=pi, dst=dst: e.tensor_copy(out=dst.t[:], in_=psb[pi][:, 0:512]), [f"ps{pi}"], [dst.k])
                    else:
                        self.A("act", lambda e, pi=pi, dst=dst: e.activation(out=dst.t[:], in_=psb[pi][:, 0:512], func=AF.Copy), [f"ps{pi}"], [dst.k])
                rel(Hbh, Hkh, Hv, Hbt, Hkt)
                yield
                NkT, Nk = N0T, N0
                for lev in range(6):
                    NT2, N2 = None, None
                    if lev < 5:
                        NT2 = ha()
                        pi = bmm(Nk, NkT)
                        self.A("act", lambda e, pi=pi, NT2=NT2: e.activation(out=NT2.t[:], in_=ps[pi][:, :], func=AF.Copy), [f"ps{pi}"], [NT2.k])
                        if lev < 4:
                            N2 = ha()
                            pi = bmm(NkT, Nk)
                            self.A("dve", lambda e, pi=pi, N2=N2: e.tensor_copy(out=N2.t[:], in_=ps[pi][:, :]), [f"ps{pi}"], [N2.k])
                        yield
                    Yn = [ha(), ha()]
                    for half in range(2):
                        pi = self.next_ps()
                        for c4 in range(4):
                            c = half * 4 + c4
                            for hh in range(2):
                                P = slice(64 * hh, 64 * hh + 64)
                                if half == 0:
                                    self.mm(ps[pi][P, c4 * 128:(c4 + 1) * 128], self.ident_b[P, P], Yb[half].t[P, c4 * 128:(c4 + 1) * 128],
                                            ["ident_b", Yb[half].k], [f"ps{pi}"], start=True, stop=False)
                                self.mm(ps[pi][P, c4 * 128:(c4 + 1) * 128], NkT.t[P, c * 64:(c + 1) * 64], Yb[half].t[P, c4 * 128:(c4 + 1) * 128],
                                        [NkT.k, Yb[half].k], [f"ps{pi}"], start=(half == 1), stop=True)
                        if half == 0:
                            self.A("act", lambda e, pi=pi, d=Yn[half]: e.activation(out=d.t[:], in_=ps[pi][:, :], func=AF.Copy), [f"ps{pi}"], [Yn[half].k])
                        else:
                            self.A("dve", lambda e, pi=pi, d=Yn[half], o=Yb[half]: e.tensor_tensor(out=d.t[:], in0=ps[pi][:, :], in1=o.t[:], op=ALU.add),
                                   [f"ps{pi}", Yb[half].k], [Yn[half].k])
                    rel(Yb[0], Yb[1])
                    Yb = Yn
                    if lev < 5:
                        rel(NkT)
                        if Nk is not None:
                            rel(Nk)
                        NkT, Nk = NT2, N2
                    yield
                rel(NkT)
                def ybmm(col0, rh):
                    pi = self.next_ps()
                    for c in range(8):
                        half, c4 = c // 4, c % 4
                        for hh in range(2):
                            P = slice(64 * hh, 64 * hh + 64)
                            self.mm(ps[pi][P, c * 64:(c + 1) * 64], Yb[half].t[P, c4 * 128 + col0:c4 * 128 + col0 + 64], rh.t[P, c * 64:(c + 1) * 64],
                                    [Yb[half].k, rh.k], [f"ps{pi}"])
                    return pi

                GT, QT, RyT, QyT = ha(), ha(), ha(), ha()
                pi = ybmm(0, Btk)
                self.A("act", lambda e, pi=pi: e.activation(out=GT.t[:], in_=ps[pi][:, :], func=AF.Copy), [f"ps{pi}"], [GT.k])
                pi = ybmm(64, Btk)
                self.A("dve", lambda e, pi=pi: e.tensor_tensor(out=QT.t[:], in0=ps[pi][:, :], in1=Ktk.t[:], op=ALU.add), [f"ps{pi}", Ktk.k], [QT.k])
                yield
                pi = ybmm(0, LrbT)
                self.A("dve", lambda e, pi=pi: e.tensor_tensor(out=RyT.t[:], in0=ps[pi][:, :], in1=Hrt.t[:], op=ALU.add), [f"ps{pi}", Hrt.k], [RyT.k])
                pi = ybmm(64, LrbT)
                self.A("dve", lambda e, pi=pi: e.tensor_tensor(out=QyT.t[:], in0=ps[pi][:, :], in1=LrkT.t[:], op=ALU.add), [f"ps{pi}", LrkT.k], [QyT.k])
                rel(Yb[0], Yb[1], Hat, Hrt, LrbT, LrkT, Btk, Ktk)
                yield
                if self.dbg and hp == 0 and g == 1:
                    for nm, b in (("GT", GT), ("QT", QT), ("RyT", RyT), ("QyT", QyT), ("Vtk", Vtk)):
                        self.dump(nm, b.t[:], [b.k], [128, 512])
                if self.stage <= 3:
                    rel(Fv, Fg, GT, QT, RyT, QyT, Vtk)
                    return

                return dict(GT=GT, QT=QT, RyT=RyT, QyT=QyT, Vtk=Vtk, Fv=Fv, Fg=Fg)

        def seq(hp, g, sidx, R):
            Hf, Hb, gst = Hf_s[sidx], Hb_s[sidx], gst_s[sidx]
            kHf, kHb, kgst = f"Hf{sidx}", f"Hb{sidx}", f"gst{sidx}"
            pcs = pcs_s[sidx][:, g * 8:(g + 1) * 8]
            kpcs = f"pcs{sidx}_{g}"
            GT, QT, RyT, QyT, Vtk, Fv, Fg = R["GT"], R["QT"], R["RyT"], R["QyT"], R["Vtk"], R["Fv"], R["Fg"]
            if True:
                gs = slice(g * 512, (g + 1) * 512)
                Yraw = fa()
                for c in range(8):
                    cs = slice(c * 64, (c + 1) * 64)
                    py, ph = self.next_ps(), self.next_ps()
                    for hh in range(2):
                        P = slice(64 * hh, 64 * hh + 64)
                        self.mm(ps[ph][P, 0:64], QT.t[P, cs], Vtk.t[P, cs], [QT.k, Vtk.k], [f"ps{ph}"], start=True, stop=False)
                        self.mm(ps[ph][P, 0:64], GT.t[P, cs], Hb[P, :], [GT.k, kHb], [f"ps{ph}"], start=False, stop=True)
                    for hh in range(2):
                        P = slice(64 * hh, 64 * hh + 64)
                        self.mm(ps[py][P, 0:64], QyT.t[P, cs], Vtk.t[P, cs], [QyT.k, Vtk.k], [f"ps{py}"], start=True, stop=False)
                        self.mm(ps[py][P, 0:64], RyT.t[P, cs], Hb[P, :], [RyT.k, kHb], [f"ps{py}"], start=False, stop=True)
                    self.A("dve", lambda e, ph=ph, c=c: e.scalar_tensor_tensor(out=Hb[:], in0=Hf[:], scalar=pcs[:, c:c + 1], in1=ps[ph][:, 0:64],
                                                                              op0=ALU.mult, op1=ALU.add), [f"ps{ph}", kHf, kpcs], [kHb])
                    self.A("dve", lambda e, ph=ph, c=c: e.scalar_tensor_tensor(out=Hf[:], in0=Hf[:], scalar=pcs[:, c:c + 1], in1=ps[ph][:, 0:64],
                                                                              op0=ALU.mult, op1=ALU.add), [f"ps{ph}", kHf, kpcs], [kHf])
                    self.A("act", lambda e, py=py, Yraw=Yraw, cs=cs: e.activation(out=Yraw.t[:, cs], in_=ps[py][:, 0:64], func=AF.Copy),
                           [f"ps{py}"], [Yraw.k])
                    yield
                rel(GT, QT, RyT, QyT, Vtk)
                if self.dbg and hp == 0 and g == 1:
                    self.dump("Yraw", Yraw.t[:], [Yraw.k], [128, 512])
                Fsq = fa()
                self.A("act", lambda e, Fsq=Fsq, Yraw=Yraw: e.activation(out=Fsq.t[:], in_=Yraw.t[:], func=AF.Square), [Yraw.k], [Fsq.k])
                self.A("dve", lambda e, Yraw=Yraw: e.tensor_reduce(out=gst[:, 0:8], in_=v3(Yraw.t[:], 64), axis=AX.X, op=ALU.add), [Yraw.k], [kgst])
                self.A("dve", lambda e, Fsq=Fsq: e.tensor_reduce(out=gst[:, 8:16], in_=v3(Fsq.t[:], 64), axis=AX.X, op=ALU.add), [Fsq.k, kgst], [kgst])
                self.A("dve", lambda e: e.tensor_scalar(out=gst[:, 16:24], in0=gst[:, 0:8], scalar1=1.0 / 64, scalar2=None, op0=ALU.mult), [kgst], [kgst])
                self.A("dve", lambda e: e.tensor_tensor(out=gst[:, 24:32], in0=gst[:, 16:24], in1=gst[:, 16:24], op=ALU.mult), [kgst], [kgst])
                self.A("dve", lambda e: e.scalar_tensor_tensor(out=gst[:, 32:40], in0=gst[:, 8:16], scalar=1.0 / 64, in1=gst[:, 24:32],
                                                                op0=ALU.mult, op1=ALU.subtract), [kgst], [kgst])
                self.A("dve", lambda e: e.tensor_scalar(out=gst[:, 32:40], in0=gst[:, 32:40], scalar1=LNX_EPS, scalar2=None, op0=ALU.add), [kgst], [kgst])
                self.A("act", lambda e: e.activation(out=gst[:, 40:48], in_=gst[:, 32:40], func=AF.Sqrt), [kgst], [kgst])
                self.A("dve", lambda e: e.reciprocal(out=gst[:, 48:56], in_=gst[:, 40:48]), [kgst], [kgst])
                self.A("dve", lambda e, Yraw=Yraw: e.tensor_tensor(out=v3(Yraw.t[:], 64), in0=v3(Yraw.t[:], 64),
                                                                   in1=gst[:, 16:24].unsqueeze(2).to_broadcast([128, 8, 64]), op=ALU.subtract),
                       [Yraw.k, kgst], [Yraw.k])
                Ynb = ha()
                self.A("dve", lambda e, Yraw=Yraw, Ynb=Ynb: e.tensor_tensor(out=v3(Ynb.t[:], 64), in0=v3(Yraw.t[:], 64),
                                                                            in1=gst[:, 48:56].unsqueeze(2).to_broadcast([128, 8, 64]), op=ALU.mult),
                       [Yraw.k, kgst], [Ynb.k])
                pi = btr(Ynb, 0, 64)
                self.A("act", lambda e, pi=pi, Fsq=Fsq: e.activation(out=Fsq.t[:], in_=psb[pi][:, 0:512], func=AF.Identity,
                                                                       scale=prm[:, LNW + hp:LNW + hp + 1], bias=prm[:, LNB + hp:LNB + hp + 1]),
                       [f"ps{pi}"] + PR2, [Fsq.k])
                self.A("dve", lambda e, Fsq=Fsq, Fv=Fv: e.tensor_tensor(out=Fsq.t[:], in0=Fsq.t[:], in1=Fv.t[:], op=ALU.add), [Fsq.k, Fv.k], [Fsq.k])
                self.A("dve", lambda e, Fsq=Fsq, Fg=Fg, gs=gs: e.tensor_tensor(out=yaT[:, hp, gs], in0=Fsq.t[:], in1=Fg.t[:], op=ALU.mult),
                       [Fsq.k, Fg.k], [f"yaT{g}"])
                rel(Yraw, Fsq, Ynb, Fv, Fg)
        def stream(hp, sidx):
            Hf, Hb = Hf_s[sidx], Hb_s[sidx]
            if hp == 0:
                slot, skey = self.pre_s0
            else:
                slot, skey = self.load_w([(hp * 128, 128), (512 + hp * 128, 128), (1024 + hp * 128, 128), (1664 + hp * 128, 128)], slot_i=sidx)
            self.A("pool", lambda e: e.memset(Hf[:], 0.0), (), [f"Hf{sidx}"])
            self.A("pool", lambda e: e.memset(Hb[:], 0.0), (), [f"Hb{sidx}"])
            R_prev = yield from batch(hp, 0, slot, skey, sidx)
            for g in range(1, NG + 1):
                if R_prev is None:
                    return
                gens = []
                res = {}
                sg_ = seq(hp, g - 1, sidx, R_prev)
                bg_ = batch(hp, g, slot, skey, sidx) if g < NG else None
                alive = [sg_, bg_] if bg_ is not None else [sg_]
                NSQ = int(os.environ.get("KNSQ", "1"))
                NBT = int(os.environ.get("KNBT", "1"))
                while alive:
                    for gen in list(alive):
                        for _rep in range(NSQ if gen is sg_ else NBT):
                            if gen not in alive:
                                break
                            try:
                                next(gen)
                                yield
                            except StopIteration as ex:
                                alive.remove(gen)
                                if gen is bg_:
                                    res["R"] = ex.value
                R_prev = res.get("R")

        self.cur = None
        ffree_s[None] = list(range(NF))
        hfree_s[None] = list(range(NH))
        for pair in range(2):
            gens = [stream(2 * pair, 0), stream(2 * pair + 1, 1)]
            STAG = int(os.environ.get("KSTAG", "12"))
            alive = [True, True]
            for _ in range(STAG):
                try:
                    next(gens[0])
                except StopIteration:
                    alive[0] = False
                    break
            while any(alive):
                for i in range(2):
                    if alive[i]:
                        try:
                            next(gens[i])
                        except StopIteration:
                            alive[i] = False
        self.q_pre = self.load_w([(2176, 512)], slot_i=0)
        self.k_pre = self.load_w([(2688, 512)], slot_i=1)
        if self.dbg:
            yd = sb("yd", [128, 4, 512], F32)
            self.A("dve", lambda e: e.tensor_copy(out=yd[:], in_=yaT[:, :, 512:1024]), [f"yaT{g}" for g in range(4)], ["yd"])
            self.dump("yaT", yd[:], ["yd"], [128, 4, 512])


    def c0(self):
        nc, prm, ps, psb, hT = self.nc, self.prm, self.ps, self.psb, self.hT
        fl, ones_f = self.fl, self.ones_f
        fl2 = lambda t: t[:].rearrange("p a b -> p (a b)")
        self.A("pool", lambda e: e.memset(ones_f[:], 1.0), (), ["ones_f"])
        slot, skey = self.wf, "wf"
        self.A("pool", lambda e: e.dma_start(out=self.wf[:, :, :], in_=self.w_in[:, 4224:4232].rearrange("(k p) c -> p k c", p=128)), (), ["wf"], dsem="wf")
        pi = self.next_ps()
        for tt in range(16):
            for kc in range(8):
                self.mm(ps[pi][:, tt * 8:(tt + 1) * 8], hT[:, kc, tt * 128:(tt + 1) * 128], slot[:, kc, 0:8],
                        [skey, f"hT{tt // 4}"], [f"ps{pi}"], start=(kc == 0), stop=(kc == 7))
        self.A("dve", lambda e, pi=pi: e.tensor_tensor(out=fl[0][:], in0=ps[pi][:, 0:128].rearrange("p (a b) -> p a b", b=8),
                                                        in1=self.fb_bc[:, :].unsqueeze(1).to_broadcast([128, 16, 8]), op=ALU.add),
               [f"ps{pi}", "params"], ["fl0"])
        self.A("act", lambda e: e.activation(out=fl2(fl[0]), in_=fl2(fl[0]), func=AF.Exp, scale=-1.0), ["fl0"], ["fl0"])
        self.A("dve", lambda e: e.tensor_scalar(out=fl2(fl[0]), in0=fl2(fl[0]), scalar1=1.0, scalar2=None, op0=ALU.add), ["fl0"], ["fl0"])
        self.A("act", lambda e: e.activation(out=fl2(fl[0]), in_=fl2(fl[0]), func=AF.Ln), ["fl0"], ["fl0"])
        ploc, ptot = self.next_ps(), self.next_ps()
        self.mm(ps[ploc][:, 0:128], self.mfox_f[:], fl2(fl[0]), ["mfox_f", "fl0"], [f"ps{ploc}"])
        self.mm(ps[ptot][:, 0:128], ones_f[:], fl2(fl[0]), ["ones_f", "fl0"], [f"ps{ptot}"])
        self.A("act", lambda e: e.activation(out=fl2(fl[1]), in_=ps[ptot][:, 0:128], func=AF.Copy), [f"ps{ptot}"], ["fl1"])
        self.A("dve", lambda e: e.tensor_copy(out=fl[2][:, 0, :], in_=fl[1][:, 0, :]), ["fl1"], ["fl2"])
        for tt in range(1, 16):
            self.A("dve", lambda e, tt=tt: e.tensor_tensor(out=fl[2][:, tt, :], in0=fl[2][:, tt - 1, :], in1=fl[1][:, tt, :], op=ALU.add),
                   ["fl1", "fl2"], ["fl2"])
        self.A("dve", lambda e: e.tensor_tensor(out=fl2(fl[3]), in0=fl2(fl[2]), in1=fl2(fl[1]), op=ALU.subtract), ["fl1", "fl2"], ["fl3"])
        self.A("dve", lambda e: e.tensor_tensor(out=fl2(fl[4]), in0=ps[ploc][:, 0:128], in1=fl2(fl[3]), op=ALU.add), [f"ps{ploc}", "fl3"], ["fl4"])
        if self.dbg:
            self.dump("Ccum", fl[4][:], ["fl4"], [128, 16, 8])

    def fox(self):
        nc, sbp, prm, ps, psb, hT = self.nc, self.sbp, self.prm, self.ps, self.psb, self.hT
        PR2 = self.PR2
        QG, KG = self.QG, self.KG
        qT = sbp("qT", [128, 4, T], BF16)
        kT = sbp("kT", [128, 4, T], BF16)
        Vx = sbp("Vx", [128, 16, 8, 65], BF16)
        sgB = sbp("sgB", [128, 4, T], BF16)
        ones_f, fl = self.ones_f, self.fl
        ft = [sbp(f"ft{i}", [128, 512], F32) for i in range(4)]
        fh = [sbp(f"fh{i}", [128, 512], BF16) for i in range(2)]
        self.A("pool", lambda e: e.memset(Vx[:, :, :, 64:65], 1.0), (), ["Vx1"])
        fl2 = lambda t: t[:].rearrange("p a b -> p (a b)")

        nft = [0]

        def ftmp():
            i = nft[0] % 4
            nft[0] += 1
            return ft[i], f"ft{i}"

        def qk_group(dst, dname, gcol, sc, epsv, slot, skey, hp, g):
            gs = slice(g * 512, (g + 1) * 512)
            pi = self.inproj_fm(slot, skey, hp * 128, g)
            fsq, ksq = ftmp()
            fraw, kraw = ftmp()
            hq = fh[(nft[0] // 2) % 2]
            khq = f"fh{(nft[0] // 2) % 2}"
            self.A("act", lambda e: e.activation(out=hq[:], in_=ps[pi][:, :], func=AF.Square), [f"ps{pi}"], [khq])
            self.A("act", lambda e: e.activation(out=fraw[:], in_=ps[pi][:, :], func=AF.Copy, scale=prm[:, gcol:gcol + 1]), [f"ps{pi}"] + PR2, [kraw])
            p2 = self.next_ps()
            self.mm(ps[p2][:, :], self.bones_b[:], hq[:], ["bones_b", khq], [f"ps{p2}"])
            self.A("dve", lambda e: e.tensor_scalar(out=fsq[:], in0=ps[p2][:, :], scalar1=sc, scalar2=epsv, op0=ALU.mult, op1=ALU.add),
                   [f"ps{p2}"], [ksq])
            self.A("act", lambda e: e.activation(out=fsq[:], in_=fsq[:], func=AF.Sqrt), [ksq], [ksq])
            self.A("dve", lambda e: e.reciprocal(out=fsq[:], in_=fsq[:]), [ksq], [ksq])
            self.A("pool", lambda e: e.tensor_tensor(out=dst[:, hp, gs], in0=fraw[:], in1=fsq[:], op=ALU.mult), [kraw, ksq], [f"{dname}{hp}_{g}"])

        slot_q = self.q_pre if hasattr(self, "q_pre") else self.load_w([(2176, 512)], slot_i=0)
        slot_k = self.k_pre if hasattr(self, "k_pre") else self.load_w([(2688, 512)], slot_i=1)
        for hp in range(4):
            for g in range(NG):
                qk_group(qT, "qT", QG, 1.0, 64 * RMS_EPS, slot_q[0], slot_q[1], hp, g)
        slot_v = self.load_w([(3200, 512)], slot_i=0)
        for hp in range(4):
            for g in range(NG):
                qk_group(kT, "kT", KG, 1.0 / 64, RMS_EPS, slot_k[0], slot_k[1], hp, g)
        slot_g = self.load_w([(3712, 512)], slot_i=1)
        slot, skey = slot_v
        for tt in range(16):
            pi = self.next_ps()
            for kc in range(8):
                self.mm(ps[pi][:, :], hT[:, kc, tt * 128:(tt + 1) * 128], slot[:, kc, 0:512], [skey, f"hT{tt // 4}"], [f"ps{pi}"],
                        start=(kc == 0), stop=(kc == 7))
            eng = "act" if tt % 2 == 0 else "dve"
            if eng == "act":
                self.A("act", lambda e, pi=pi, tt=tt: e.activation(out=Vx[:, tt, :, 0:64], in_=ps[pi][:, :].rearrange("p (h d) -> p h d", d=64),
                                                                    func=AF.Copy), [f"ps{pi}"], [f"Vx{tt}"])
            else:
                self.A("dve", lambda e, pi=pi, tt=tt: e.tensor_copy(out=Vx[:, tt, :, 0:64], in_=ps[pi][:, :].rearrange("p (h d) -> p h d", d=64)),
                       [f"ps{pi}"], [f"Vx{tt}"])
        slot, skey = slot_g
        for hp in range(4):
            for g in range(NG):
                pi = self.inproj_fm(slot, skey, hp * 128, g)
                self.A("act", lambda e, pi=pi, hp=hp, g=g: e.activation(out=sgB[:, hp, g * 512:(g + 1) * 512], in_=ps[pi][:, :], func=AF.Silu),
                       [f"ps{pi}"], [f"sgB{g}"])
        if self.dbg:
            qd = sbp("qd", [128, 4, 256], F32)
            kd = sbp("kd", [128, 4, 256], F32)
            self.A("dve", lambda e: e.tensor_copy(out=qd[:], in_=qT[:, :, 0:256]), [f"qT{hp}_0" for hp in range(4)], ["qd"])
            self.A("dve", lambda e: e.tensor_copy(out=kd[:], in_=kT[:, :, 0:256]), [f"kT{hp}_0" for hp in range(4)], ["kd"])
            self.dump("qT", qd[:], ["qd"], [128, 4, 256])
            self.dump("kT", kd[:], ["kd"], [128, 4, 256])
        self.mslots_pre = {0: self.load_w([(4232, 256), (5256, 256)], slot_i=0), 1: self.load_w([(4232 + 256, 256), (5256 + 256, 256)], slot_i=1)}
        R1s = [sbp(f"R1_{i}", [1, 512], BF16) for i in range(4)]
        ones1 = sbp("ones1", [1, 128], BF16)
        self.A("pool", lambda e: e.memset(ones1[:], 1.0), (), ["ones1"])
        NPT = 4
        Pt = [sbp(f"Pt{i}", [128, 512], BF16) for i in range(NPT)]
        otok = [sbp(f"otok{i}", [128, 4, 512], BF16) for i in range(2)]
        recs = sbp("recs", [128, 32], F32)
        items = [(G, h, kb) for G in range(4) for h in range(8) for kb in range(4 * G + 4)]
        LA = 2

        def s1(i):
            G, h, kb = items[i]
            hp = h // 2
            P = slice(64 * (h % 2), 64 * (h % 2) + 64)
            gh = G * 8 + h
            pa = 4 + (gh % 2)
            r1 = R1s[gh % 4]
            r1k = f"R1_{gh % 4}"
            if kb == 0:
                self.A("dve", lambda e: e.memset(ps[pa][:, 0:260], 0.0), (), [f"ps{pa}"])
                self.A("dve", lambda e: e.tensor_scalar(
                    out=r1[0:1, :].rearrange("p (q t) -> p q t", t=128),
                    in0=fl[2][0:1, 4 * G:4 * G + 4, h:h + 1].to_broadcast([1, 4, 128]),
                    scalar1=-1.0, scalar2=None, op0=ALU.mult), ["fl2"], [r1k])
            j0 = max(kb - 4 * G, 0)
            c0 = j0 * 128
            pi = i % 4
            pt = Pt[i % NPT]
            ptk = f"Pt{i % NPT}"
            self.mm(ps[pi][:, c0:512], kT[P, hp, kb * 128:(kb + 1) * 128], qT[P, hp, G * 512 + c0:(G + 1) * 512],
                    [f"kT{hp}_{kb // 4}", f"qT{hp}_{G}"], [f"ps{pi}"], start=True, stop=False)
            self.mm(ps[pi][:, c0:512], ones1[0:1, :], r1[0:1, c0:512], ["ones1", r1k], [f"ps{pi}"], start=False, stop=True)
            self.A("act", lambda e: e.activation(out=pt[:, c0:512], in_=ps[pi][:, c0:512], func=AF.Exp, bias=fl[4][:, kb, h:h + 1]),
                   [f"ps{pi}", "fl4"], [ptk])
            if kb >= 4 * G:
                self.A("pool", lambda e: e.tensor_tensor(out=pt[:, c0:c0 + 128], in0=pt[:, c0:c0 + 128], in1=self.M_fox[:], op=ALU.mult),
                       [ptk, "M_fox"], [ptk])

        def s2(i):
            G, h, kb = items[i]
            gh = G * 8 + h
            pa = 4 + (gh % 2)
            rc = (gh % 8) * 4
            ot = otok[G % 2]
            otk = f"otok{G % 2}"
            j0 = max(kb - 4 * G, 0)
            pt = Pt[i % NPT]
            ptk = f"Pt{i % NPT}"
            for j in range(j0, 4):
                self.mm(ps[pa][:, j * 65:(j + 1) * 65], pt[:, j * 128:(j + 1) * 128], Vx[:, kb, h, :], [ptk, f"Vx{kb}", "Vx1"], [f"ps{pa}"],
                        start=False, stop=(kb == 4 * G + j), skip=True)
            if kb == 4 * G + 3:
                self.A("dve", lambda e: e.reciprocal(out=recs[:, rc:rc + 4], in_=ps[pa][:, 0:260].rearrange("p (j d) -> p j d", d=65)[:, :, 64]),
                       [f"ps{pa}"], [f"recs{rc}"])
                self.A("dve", lambda e: e.tensor_tensor(
                    out=ot[:, :, h * 64:(h + 1) * 64], in0=ps[pa][:, 0:260].rearrange("p (j d) -> p j d", d=65)[:, :, 0:64],
                    in1=recs[:, rc:rc + 4].unsqueeze(2).to_broadcast([128, 4, 64]), op=ALU.mult), [f"ps{pa}", f"recs{rc}"], [otk])
                if h == 7:
                    for j in range(4):
                        qb = 4 * G + j
                        qs = slice(qb * 128, (qb + 1) * 128)
                        ptr = 6 + (qb % 2)
                        for hp in range(4):
                            self.tr(psb[ptr][:, hp * 128:(hp + 1) * 128], ot[:, j, hp * 128:(hp + 1) * 128], self.ident_b[:], [otk, "ident_b"], [f"ps{ptr}"])
                        self.A("dve", lambda e, ptr=ptr, qs=qs: e.tensor_tensor(
                            out=self.ybT[:, :, qs], in0=psb[ptr][:, 0:512].rearrange("p (a t) -> p a t", t=128), in1=sgB[:, :, qs], op=ALU.mult),
                            [f"ps{ptr}"] + [f"sgB{G}"], [f"ybT{G}"])

        n_it = len(items)
        for i in range(n_it + LA):
            if i < n_it:
                s1(i)
            if i - LA >= 0:
                s2(i - LA)
        if self.dbg:
            ybd = sbp("ybd", [128, 4, 512], F32)
            self.A("dve", lambda e: e.tensor_copy(out=ybd[:], in_=self.ybT[:, :, 512:1024]), ["ybT1"], ["ybd"])
            self.dump("ybT", ybd[:], ["ybd"], [128, 4, 512])

    def merge_out(self):
        nc, sbp, prm, ps, psb, hT = self.nc, self.sbp, self.prm, self.ps, self.psb, self.hT
        yaT, ybT = self.yaT, self.ybT
        mT = sbp("mT", [128, 8, T], BF16)
        woa, wob = self.woa, self.wob
        wo = sbp("wo", [128, 8, D], BF16)
        fm = [sbp(f"fm{i}", [128, 512], F32) for i in range(4)]
        mslots = dict(self.mslots_pre)
        self.A("pool", lambda e: e.dma_start(out=wo[:], in_=self.w_out.rearrange("(k p) c -> p k c", p=128)), (), ["wo"], dsem="wo")
        cnt = [0]

        def mgroup(slot, skey, dd, dc, g):
            gs = slice(g * 512, (g + 1) * 512)
            ia, ib = (cnt[0] % 2) * 2, (cnt[0] % 2) * 2 + 1
            cnt[0] += 1
            fa_, fb_ = fm[ia], fm[ib]
            ka, kb_ = f"fm{ia}", f"fm{ib}"
            p1 = self.inproj_fm(slot, skey, dd * 128, g)
            self.A("act", lambda e: e.activation(out=fa_[:], in_=ps[p1][:, :], func=AF.Sigmoid), [f"ps{p1}"], [ka])
            p2 = self.inproj_fm(slot, skey, 256 + dd * 128, g)
            self.A("act", lambda e: e.activation(out=fb_[:], in_=ps[p2][:, :], func=AF.Sigmoid), [f"ps{p2}"], [kb_])
            p3 = self.next_ps()
            for hp in range(4):
                self.mm(ps[p3][:, :], woa[:, hp, dc * 128:(dc + 1) * 128], yaT[:, hp, gs], ["woa", f"yaT{g}"], [f"ps{p3}"],
                        start=(hp == 0), stop=(hp == 3))
            p4 = self.next_ps()
            for hp in range(4):
                self.mm(ps[p4][:, :], wob[:, hp, dc * 128:(dc + 1) * 128], ybT[:, hp, gs], ["wob", f"ybT{g}"], [f"ps{p4}"],
                        start=(hp == 0), stop=(hp == 3))
            self.A("dve", lambda e: e.tensor_tensor(out=fa_[:], in0=ps[p3][:, :], in1=fa_[:], op=ALU.mult), [f"ps{p3}", ka], [ka])
            self.A("dve", lambda e: e.tensor_tensor(out=fb_[:], in0=ps[p4][:, :], in1=fb_[:], op=ALU.mult), [f"ps{p4}", kb_], [kb_])
            self.A("pool", lambda e: e.tensor_tensor(out=mT[:, dc, gs], in0=fa_[:], in1=fb_[:], op=ALU.add), [ka, kb_], [f"mT{g}"])

        for dcp in range(4):
            slot, skey = mslots[dcp]
            for dd in range(2):
                for g in range(NG):
                    mgroup(slot, skey, dd, dcp * 2 + dd, g)
            if dcp + 2 < 4:
                mslots[dcp + 2] = self.load_w([(4232 + (dcp + 2) * 256, 256), (5256 + (dcp + 2) * 256, 256)], slot_i=dcp % 2)
        if self.dbg:
            md = sbp("md", [128, 8, 256], F32)
            self.A("dve", lambda e: e.tensor_copy(out=md[:], in_=mT[:, :, 512:768]), ["mT1"], ["md"])
            self.dump("mT", md[:], ["md"], [128, 8, 256])
        NXR = 4
        xr = [sbp(f"xr{i}", [128, D], F32) for i in range(NXR)]
        oo = [sbp(f"oo{i}", [128, D], F32) for i in range(2)]
        junk2 = sbp("junk2", [128, D], BF16)
        st2 = sbp("st2", [128, 64], F32)
        x, out = self.x, self.out

        def ftile(tt):
            xt, xk = xr[tt % NXR], f"xr{tt % NXR}"
            ot, ok = oo[tt % 2], f"oo{tt % 2}"
            self.A("sp", lambda e: e.dma_start(out=xt[:], in_=x[tt * 128:(tt + 1) * 128, :]), (), [xk], dsem=xk)
            for half in range(2):
                pi = self.next_ps()
                for kc in range(8):
                    self.mm(ps[pi][:, :], mT[:, kc, tt * 128:(tt + 1) * 128], wo[:, kc, half * 512:(half + 1) * 512],
                            [f"mT{tt // 4}", "wo"], [f"ps{pi}"], start=(kc == 0), stop=(kc == 7))
                self.A("dve", lambda e, pi=pi, half=half: e.tensor_tensor(out=xt[:, half * 512:(half + 1) * 512], in0=ps[pi][:, :],
                                                                          in1=xt[:, half * 512:(half + 1) * 512], op=ALU.add),
                       [f"ps{pi}", xk], [xk])
            self.A("act", lambda e: e.activation(out=junk2[:], in_=xt[:], func=AF.Square, accum_out=st2[:, tt:tt + 1]), [xk], ["junk2", f"s2a{tt}"])
            self.A("dve", lambda e: e.tensor_scalar(out=st2[:, 16 + tt:17 + tt], in0=st2[:, tt:tt + 1], scalar1=1.0 / D, scalar2=RMS_EPS,
                                                    op0=ALU.mult, op1=ALU.add), [f"s2a{tt}"], [f"s2b{tt}"])
            self.A("act", lambda e: e.activation(out=st2[:, 32 + tt:33 + tt], in_=st2[:, 16 + tt:17 + tt], func=AF.Sqrt), [f"s2b{tt}"], [f"s2c{tt}"])
            self.A("dve", lambda e: e.reciprocal(out=st2[:, 48 + tt:49 + tt], in_=st2[:, 32 + tt:33 + tt]), [f"s2c{tt}"], [f"s2d{tt}"])
            self.A("dve", lambda e: e.scalar_tensor_tensor(out=ot[:], in0=xt[:], scalar=st2[:, 48 + tt:49 + tt], in1=self.fing_bc[:],
                                                           op0=ALU.mult, op1=ALU.mult), [xk, f"s2d{tt}", "params"], [ok])
            self.A("pool", lambda e: e.dma_start(out=out[tt * 128:(tt + 1) * 128, :], in_=ot[:]), [ok], [f"outw{tt}"], dsem=f"oo{tt % 2}")
            self.final_reads.append(f"outw{tt}")

        for tt in range(16):
            ftile(tt)

    def finish(self, out):
        if self.stage < 99:
            z = self.sbp("zout", [128, 8], F32)
            self.A("pool", lambda e: e.memset(z[:], 0.0), (), ["zout"])
            self.A("sp", lambda e: e.dma_start(out=out[0:128, 0:8], in_=z[:]), ["zout"], ["o_final"], dsem="o_final")
            self.final_reads.append("o_final")
        self.A("sp", lambda e: None, self.final_reads, ())
        self.S.emit(self.nc, self.st)
        self.ph.close()
        self.st.close()
        return self.nc


INPUT_ORDER = ["x", "norm_g", "w_in", "shift_mu", "w_lora_up", "w0", "a_lora_up", "a0", "k_k", "k_a", "r_k",
               "lnx_w", "lnx_b", "f_bias", "q_norm_g", "k_norm_g", "w_out_a", "w_out_b", "w_out", "final_norm_g"]


def make_in_maps(inputs, ncores=8):
    f = lambda a: np.ascontiguousarray(np.asarray(a, dtype=np.float32))
    shared = {
        "norm_g": f(inputs["norm_g"]).reshape(1, D),
        "w_in": f(inputs["w_in"]).reshape(D, IN_COLS),
        "shift_mu": f(inputs["shift_mu"]).reshape(1, 2176),
        "w_lora_up": f(inputs["w_lora_up"]).reshape(64, 512),
        "w0": f(inputs["w0"]).reshape(1, 512),
        "a_lora_up": f(inputs["a_lora_up"]).reshape(64, 512),
        "a0": f(inputs["a0"]).reshape(1, 512),
        "k_k": f(inputs["k_k"]).reshape(1, 512),
        "k_a": f(inputs["k_a"]).reshape(1, 512),
        "r_k": f(inputs["r_k"]).reshape(1, 512),
        "lnx_w": f(inputs["lnx_w"]).reshape(1, 512),
        "lnx_b": f(inputs["lnx_b"]).reshape(1, 512),
        "f_bias": f(inputs["f_bias"]).reshape(1, 8),
        "q_norm_g": f(inputs["q_norm_g"]).reshape(1, 64),
        "k_norm_g": f(inputs["k_norm_g"]).reshape(1, 64),
        "w_out_a": f(inputs["w_out_a"]).reshape(512, D),
        "w_out_b": f(inputs["w_out_b"]).reshape(512, D),
        "w_out": f(inputs["w_out"]).reshape(D, D),
        "final_norm_g": f(inputs["final_norm_g"]).reshape(1, D),
    }
    xs = f(inputs["x"])
    maps = []
    for c in range(ncores):
        m = dict(shared)
        m["x"] = np.ascontiguousarray(xs[c])
        maps.append(m)
    return maps


def kernel(**inputs):
    b = Builder()
    nc = b.build()
    in_maps = make_in_maps(inputs)
    res = run_bass_kernel_spmd(nc, in_maps, core_ids=list(range(8)))
    return np.stack([np.asarray(r["out"], dtype=np.float32) for r in res.results], axis=0)
```
